# Optimizing a Trainium2 kernel written in Bass

```python
import jax, jax.numpy as jnp
from jax import lax
import numpy as np

D_MODEL = 1024
BATCH = 1
SEQ = 16384
DEPTH = 1

POOL_WIDTH = D_MODEL // 2
POOL_WINDOWS = (2, 4, 8, 16)
N_POOL_GROUPS = len(POOL_WINDOWS)
POOL_GROUP = POOL_WIDTH // N_POOL_GROUPS
HEAD_DIM = 64
N_HEADS = (D_MODEL - POOL_WIDTH) // HEAD_DIM
N_KV = 2
GQA_GROUP = N_HEADS // N_KV
KV_WIDTH = N_KV * HEAD_DIM
N_BRANCH = 3
CMP_LEN = 32
CMP_STRIDE = 16
CMP_HIDDEN = 4 * HEAD_DIM
SEL_BLOCK = 64
N_SEL = 16
WINDOW = 512
Q_BLOCK = 128
MIX_WIDTH = POOL_WIDTH + N_HEADS * HEAD_DIM
IN_WIDTH = POOL_WIDTH + N_HEADS * HEAD_DIM + 6 * KV_WIDTH + N_BRANCH * N_HEADS
D_FF = 4 * D_MODEL
NEG_INF = -1e30
FORCE_BONUS = 1e4
EPS = 1e-6

kernel_name = "hybrid_pool_nsa_adaln_block"


def rms_norm(x, g):
    xf = x.astype(jnp.float32)
    y = xf * lax.rsqrt(jnp.mean(xf * xf, axis=-1, keepdims=True) + EPS)
    return (y * g.astype(jnp.float32)).astype(x.dtype)


def alibi_slopes(n):
    return jnp.asarray([2.0 ** (-8.0 * (h + 1) / n) for h in range(n)], jnp.float32)


def causal_multiscale_pool(u, w_pool, pool_scale):
    B, S, _ = u.shape
    ug = u.reshape(B, S, N_POOL_GROUPS, POOL_GROUP)
    cs = jnp.cumsum(ug.astype(jnp.float32), axis=1)
    cs = jnp.concatenate([jnp.zeros_like(cs[:, :1]), cs], axis=1)
    t = jnp.arange(S)
    means = []
    for gi, w in enumerate(POOL_WINDOWS):
        lo = jnp.maximum(t + 1 - w, 0)
        win_sum = cs[:, t + 1, gi] - cs[:, lo, gi]
        cnt = (t + 1 - lo).astype(jnp.float32)[None, :, None]
        means.append(win_sum / cnt)
    pooled = jnp.stack(means, axis=2).astype(u.dtype) - ug
    y = jnp.einsum('bsgc,gcd->bsgd', pooled, w_pool)
    return y.reshape(B, S, POOL_WIDTH) * pool_scale


def compress_blocks(kv, pos_emb, w1, b1, w2, b2):
    B, S = kv.shape[0], kv.shape[1]
    n_cmp = (S - CMP_LEN) // CMP_STRIDE + 1
    idx = jnp.arange(n_cmp)[:, None] * CMP_STRIDE + jnp.arange(CMP_LEN)[None, :]
    blocks = kv[:, idx] + pos_emb[:, None, :]
    flat = blocks.transpose(0, 1, 3, 2, 4).reshape(B, n_cmp, N_KV, CMP_LEN * HEAD_DIM)
    hid = jax.nn.gelu(flat @ w1 + b1)
    return hid @ w2 + b2


def nsa_attention(q, k_cmp, v_cmp, k_sel, v_sel, k_win, v_win, gates):
    B, S = q.shape[0], q.shape[1]
    n_cmp = k_cmp.shape[1]
    n_blk = S // SEL_BLOCK
    n_top = min(N_SEL, n_blk)
    n_qb = S // Q_BLOCK
    scale = HEAD_DIM ** -0.5
    slopes_gr = alibi_slopes(N_HEADS).reshape(N_KV, GQA_GROUP)

    cmp_start = jnp.arange(n_cmp) * CMP_STRIDE
    cmp_end = cmp_start + CMP_LEN - 1
    sel_start = jnp.arange(n_blk) * SEL_BLOCK
    overlap = jnp.clip(jnp.minimum(cmp_start[:, None] + CMP_LEN, sel_start[None, :] + SEL_BLOCK)
                       - jnp.maximum(cmp_start[:, None], sel_start[None, :]), 0, None)
    overlap = (overlap / CMP_LEN).astype(jnp.float32)

    qg = q.reshape(B, S, N_KV, GQA_GROUP, HEAD_DIM)
    gg = gates.reshape(B, S, N_KV, GQA_GROUP, N_BRANCH)
    k_sel_b = k_sel.reshape(B, n_blk, SEL_BLOCK, N_KV, HEAD_DIM).transpose(0, 3, 1, 2, 4)
    v_sel_b = v_sel.reshape(B, n_blk, SEL_BLOCK, N_KV, HEAD_DIM).transpose(0, 3, 1, 2, 4)
    pad = jnp.zeros((B, WINDOW, N_KV, HEAD_DIM), k_win.dtype)
    k_win_p = jnp.concatenate([pad, k_win], axis=1)
    v_win_p = jnp.concatenate([pad.astype(v_win.dtype), v_win], axis=1)
    b_ix = jnp.arange(B)[:, None, None, None]
    g_ix = jnp.arange(N_KV)[None, :, None, None]
    blk = jnp.arange(n_blk)

    def block_fn(qb):
        q0 = qb * Q_BLOCK
        t = q0 + jnp.arange(Q_BLOCK)
        qblk = lax.dynamic_slice_in_dim(qg, q0, Q_BLOCK, axis=1)
        gblk = lax.dynamic_slice_in_dim(gg, q0, Q_BLOCK, axis=1)

        d_c = (t[:, None] - cmp_end[None, :]).astype(jnp.float32)
        ok_c = d_c >= 0
        s_c = (jnp.einsum('btgrd,bngd->bgrtn', qblk, k_cmp, preferred_element_type=jnp.float32) * scale
               - slopes_gr[:, :, None, None] * d_c)
        s_c = jnp.where(ok_c, s_c, NEG_INF)
        p_c = jnp.where(ok_c, jax.nn.softmax(s_c, axis=-1), 0.0)
        o_c = jnp.einsum('bgrtn,bngd->btgrd', p_c.astype(v_cmp.dtype), v_cmp)

        imp = jnp.einsum('bgrtn,nj->bgtj', p_c, overlap)
        cur = t // SEL_BLOCK
        causal = blk[None, :] <= cur[:, None]
        forced = (blk[None, :] == 0) | (blk[None, :] == cur[:, None]) | (blk[None, :] == cur[:, None] - 1)
        imp = jnp.where(causal, imp + FORCE_BONUS * forced, -1.0)
        top_val, top_idx = lax.top_k(imp, n_top)
        ok_blk = top_val >= 0

        k_g = k_sel_b[b_ix, g_ix, top_idx]
        v_g = v_sel_b[b_ix, g_ix, top_idx]
        pos = top_idx[..., None] * SEL_BLOCK + jnp.arange(SEL_BLOCK)
        d_s = (t[None, None, :, None, None] - pos).astype(jnp.float32)
        ok_s = (d_s >= 0) & ok_blk[..., None]
        s_s = (jnp.einsum('btgrd,bgtnld->bgrtnl', qblk, k_g, preferred_element_type=jnp.float32) * scale
               - slopes_gr[None, :, :, None, None, None] * d_s[:, :, None])
        s_s = jnp.where(ok_s[:, :, None], s_s, NEG_INF)
        p_s = jax.nn.softmax(s_s.reshape(B, N_KV, GQA_GROUP, Q_BLOCK, n_top * SEL_BLOCK), axis=-1)
        p_s = p_s.reshape(s_s.shape)
        o_s = jnp.einsum('bgrtnl,bgtnld->btgrd', p_s.astype(v_g.dtype), v_g)

        k_w = lax.dynamic_slice_in_dim(k_win_p, q0, Q_BLOCK + WINDOW, axis=1)
        v_w = lax.dynamic_slice_in_dim(v_win_p, q0, Q_BLOCK + WINDOW, axis=1)
        src = q0 - WINDOW + jnp.arange(Q_BLOCK + WINDOW)
        d_w = t[:, None] - src[None, :]
        ok_w = (d_w >= 0) & (d_w < WINDOW) & (src[None, :] >= 0)
        s_w = (jnp.einsum('btgrd,bkgd->bgrtk', qblk, k_w, preferred_element_type=jnp.float32) * scale
               - slopes_gr[:, :, None, None] * d_w.astype(jnp.float32))
        s_w = jnp.where(ok_w, s_w, NEG_INF)
        p_w = jax.nn.softmax(s_w, axis=-1)
        o_w = jnp.einsum('bgrtk,bkgd->btgrd', p_w.astype(v_w.dtype), v_w)

        return gblk[..., 0:1] * o_c + gblk[..., 1:2] * o_s + gblk[..., 2:3] * o_w

    out = lax.map(block_fn, jnp.arange(n_qb))
    return out.transpose(1, 0, 2, 3, 4, 5).reshape(B, S, N_HEADS * HEAD_DIM)


def hybrid_mixer(h, w_in, w_pool, pool_scale, q_gain, kc_gain, ks_gain, kw_gain, cmp_k, cmp_v, w_out):
    B, S, _ = h.shape
    proj = h @ w_in
    splits = list(np.cumsum([POOL_WIDTH, N_HEADS * HEAD_DIM] + [KV_WIDTH] * 6))
    u, q, kc, vc, ksl, vsl, kw, vw, g = jnp.split(proj, splits, axis=-1)
    kv4 = lambda a: a.reshape(B, S, N_KV, HEAD_DIM)
    q = rms_norm(q.reshape(B, S, N_HEADS, HEAD_DIM), q_gain)
    k_cmp = rms_norm(compress_blocks(kv4(kc), *cmp_k), kc_gain)
    v_cmp = compress_blocks(kv4(vc), *cmp_v)
    k_sel = rms_norm(kv4(ksl), ks_gain)
    k_win = rms_norm(kv4(kw), kw_gain)
    gates = jax.nn.sigmoid(g.reshape(B, S, N_HEADS, N_BRANCH))
    pool_out = causal_multiscale_pool(u, w_pool, pool_scale)
    attn_out = nsa_attention(q, k_cmp, v_cmp, k_sel, kv4(vsl), k_win, kv4(vw), gates)
    return jnp.concatenate([pool_out, attn_out], axis=-1) @ w_out


def setup_inputs(seed: int = 0) -> dict:
    key = jax.random.key(seed)
    ks = jax.random.split(key, 26)
    L = DEPTH
    f32 = jnp.float32
    nrm = lambda k, shape, fan_in: jax.random.normal(k, shape, f32) * fan_in ** -0.5
    gain = lambda k, shape: 1.0 + 0.02 * jax.random.normal(k, shape, f32)
    small = lambda k, shape, s: s * jax.random.normal(k, shape, f32)
    return {
        "x": jax.random.normal(ks[0], (BATCH, SEQ, D_MODEL), f32),
        "c": jax.random.normal(ks[1], (BATCH, D_MODEL), f32),
        "w_ada": 0.5 * nrm(ks[2], (L, D_MODEL, 6 * D_MODEL), D_MODEL),
        "b_ada": small(ks[3], (L, 6 * D_MODEL), 0.01),
        "norm1_g": gain(ks[4], (L, D_MODEL)),
        "norm2_g": gain(ks[5], (L, D_MODEL)),
        "w_in": nrm(ks[6], (L, D_MODEL, IN_WIDTH), D_MODEL),
        "w_pool": nrm(ks[7], (L, N_POOL_GROUPS, POOL_GROUP, POOL_GROUP), POOL_GROUP),
        "pool_scale": gain(ks[8], (L, POOL_WIDTH)),
        "q_gain": gain(ks[9], (L, HEAD_DIM)),
        "kc_gain": gain(ks[10], (L, HEAD_DIM)),
        "ks_gain": gain(ks[11], (L, HEAD_DIM)),
        "kw_gain": gain(ks[12], (L, HEAD_DIM)),
        "cmp_pos_k": small(ks[13], (L, CMP_LEN, HEAD_DIM), 0.1),
        "cmp_w1_k": nrm(ks[14], (L, CMP_LEN * HEAD_DIM, CMP_HIDDEN), CMP_LEN * HEAD_DIM),
        "cmp_b1_k": small(ks[15], (L, CMP_HIDDEN), 0.01),
        "cmp_w2_k": nrm(ks[16], (L, CMP_HIDDEN, HEAD_DIM), CMP_HIDDEN),
        "cmp_b2_k": small(ks[17], (L, HEAD_DIM), 0.01),
        "cmp_pos_v": small(ks[18], (L, CMP_LEN, HEAD_DIM), 0.1),
        "cmp_w1_v": nrm(ks[19], (L, CMP_LEN * HEAD_DIM, CMP_HIDDEN), CMP_LEN * HEAD_DIM),
        "cmp_b1_v": small(ks[20], (L, CMP_HIDDEN), 0.01),
        "cmp_w2_v": nrm(ks[21], (L, CMP_HIDDEN, HEAD_DIM), CMP_HIDDEN),
        "cmp_b2_v": small(ks[22], (L, HEAD_DIM), 0.01),
        "w_out": nrm(ks[23], (L, MIX_WIDTH, D_MODEL), MIX_WIDTH),
        "w_ff1": nrm(ks[24], (L, D_MODEL, D_FF), D_MODEL),
        "w_ff2": nrm(ks[25], (L, D_FF, D_MODEL), D_FF),
    }


def reference(x, c, w_ada, b_ada, norm1_g, norm2_g, w_in, w_pool, pool_scale, q_gain, kc_gain,
              ks_gain, kw_gain, cmp_pos_k, cmp_w1_k, cmp_b1_k, cmp_w2_k, cmp_b2_k, cmp_pos_v,
              cmp_w1_v, cmp_b1_v, cmp_w2_v, cmp_b2_v, w_out, w_ff1, w_ff2):
    for l in range(DEPTH):
        mod = c @ w_ada[l] + b_ada[l]
        sh1, sc1, ga1, sh2, sc2, ga2 = [m[:, None, :] for m in jnp.split(mod, 6, axis=-1)]
        h = rms_norm(x, norm1_g[l]) * (1.0 + sc1) + sh1
        cmp_k = (cmp_pos_k[l], cmp_w1_k[l], cmp_b1_k[l], cmp_w2_k[l], cmp_b2_k[l])
        cmp_v = (cmp_pos_v[l], cmp_w1_v[l], cmp_b1_v[l], cmp_w2_v[l], cmp_b2_v[l])
        x = x + ga1 * hybrid_mixer(h, w_in[l], w_pool[l], pool_scale[l], q_gain[l], kc_gain[l],
                                   ks_gain[l], kw_gain[l], cmp_k, cmp_v, w_out[l])
        h = rms_norm(x, norm2_g[l]) * (1.0 + sc2) + sh2
        x = x + ga2 * (jnp.square(jax.nn.relu(h @ w_ff1[l])) @ w_ff2[l])
    return x
```

```python
import numpy as np
import ml_dtypes
from contextlib import ExitStack
import concourse.bass as bass
import concourse.mybir as mybir
from concourse.bass_utils import run_bass_kernel_spmd

F32 = mybir.dt.float32
BF16 = mybir.dt.bfloat16
AF = mybir.ActivationFunctionType
ALU = mybir.AluOpType
AX = mybir.AxisListType

S = 16384
D = 1024
NT = 128
NCORES = 8
NOWN = 16
EPS = 1e-6
NEG = -50000.0
DFF = 4096


class Buf:
    __slots__ = ("t", "w", "r", "dsem", "dcnt", "name", "excl")

    def __init__(self, t, name="", excl=False):
        self.excl = excl
        self.t = t
        self.w = {}
        self.r = {}
        self.dsem = None
        self.dcnt = 0
        self.name = name

    def __getitem__(self, k):
        return self.t[k]


class _Stop(Exception):
    pass


class Prog:
    def __init__(self, nc, es):
        self.nc = nc
        self.es = es
        self.eng = {"pe": nc.tensor, "act": nc.scalar, "dve": nc.vector, "pool": nc.gpsimd, "sp": nc.sync}
        self.sem = {}
        self.cnt = {}
        for k in self.eng:
            self.sem[k] = es.enter_context(nc.semaphore("s_" + k))
            self.cnt[k] = 0
        self.waited = {k: {} for k in self.eng}
        self.nd = 0
        self.nwaits = 0
        self.nins = 0
        self.dlast = {}

    def sb(self, name, shape, dt, es=None):
        self.nt = getattr(self, "nt", 0) + 1
        if not hasattr(self, "names"):
            self.names = {}
        self.names[name] = "t%d_%s" % (self.nt, name)
        t = (es or self.es).enter_context(self.nc.sbuf_tensor("t%d_%s" % (self.nt, name), list(shape), dt))
        return Buf(t, name)

    def ps(self, name, shape, dt=F32):
        t = self.es.enter_context(self.nc.psum_tensor(name, list(shape), dt))
        return Buf(t, name, excl=True)

    def _deps(self, reads, writes):
        d = {}
        for b in reads:
            for k, v in b.w.items():
                if d.get(k, 0) < v:
                    d[k] = v
        for b in writes:
            for k, v in b.w.items():
                if d.get(k, 0) < v:
                    d[k] = v
            for k, v in b.r.items():
                if d.get(k, 0) < v:
                    d[k] = v
        return d

    def _emit_waits(self, e, deps):
        eng = self.eng[e]
        wd = self.waited[e]
        for k, v in deps.items():
            if k == e and e == "pe":
                continue
            if wd.get(k, 0) >= v:
                continue
            eng.wait_ge(self.sem[k], v)
            self.nwaits += 1
            wd[k] = v

    stopped = False

    def op(self, e, fn, reads=(), writes=()):
        if self.stopped:
            return 0
        if any(b.excl for b in reads):
            writes = list(writes) + [b for b in reads if b.excl and b not in writes]
            reads = [b for b in reads if not b.excl]
        self._emit_waits(e, self._deps(reads, writes))
        ins = fn(self.eng[e])
        ins.then_inc(self.sem[e], 1)
        self.cnt[e] += 1
        self.nins += 1
        tok = self.cnt[e]
        for b in writes:
            b.w = {e: tok}
            b.r = {}
        for b in reads:
            if b.r.get(e, 0) < tok:
                b.r[e] = tok
        return tok

    def dma(self, q, out_ap, in_ap, reads=(), writes=(), owner=None, nowaw=False):
        if self.stopped:
            return
        tgt = owner if owner is not None else (writes[0] if writes else reads[0])
        deps = self._deps(reads, writes)
        if nowaw and tgt.dsem is not None:
            deps.pop(tgt.dsem, None)
        self._emit_waits(q, deps)
        if tgt.dsem is None:
            self.nd += 1
            key = "d%d" % self.nd
            self.sem[key] = self.es.enter_context(self.nc.semaphore(key))
            tgt.dsem = key
        key = tgt.dsem
        ins = self.eng[q].dma_start(out=out_ap, in_=in_ap)
        ins.then_inc(self.sem[key], 16)
        tgt.dcnt += 16
        self.nins += 1
        tok = tgt.dcnt
        self.dlast[key] = tok
        for b in writes:
            b.w = {key: tok}
            b.r = {}
        for b in reads:
            if not nowaw and b.r.get(key, 0) < tok:
                b.r[key] = tok

    def barrier(self):
        if self.stopped:
            return
        deps = {k: v for k, v in self.cnt.items() if v > 0}
        deps.update(self.dlast)
        for e in self.eng:
            self._emit_waits(e, dict(deps))

    def finish(self, e, bufs):
        if self.stopped:
            return
        deps = {}
        for b in bufs:
            for dd in (b.w, b.r):
                for k, v in dd.items():
                    if deps.get(k, 0) < v:
                        deps[k] = v
        self._emit_waits(e, deps)


def _slopes():
    return np.array([2.0 ** (-(h + 1)) for h in range(8)], np.float64)


def cmp_chunks(i):
    nci = (i + 2) // 2
    return sorted(set(list(range(nci)) + [7]))


def sel_slots(i):
    return list(range(0, 8 * i + 1)) + list(range(121, 128))


def core_tables(c):
    sl = _slopes()
    ql = np.arange(128)
    T = {}
    t1 = np.zeros((NOWN, 128, 256), np.float32)
    jp = np.arange(256)
    jg = (jp + 2 * c) % 256
    for i in range(NOWN):
        t = 128 * (8 * i + c) + ql
        cur = t // 64
        causal = jg[None, :] <= cur[:, None]
        forced = (jg[None, :] == 0) | (jg[None, :] == cur[:, None]) | (jg[None, :] == cur[:, None] - 1)
        t1[i] = np.where(causal, 1e4 * forced, -1.0)
    T["t1"] = t1
    cm = np.zeros((NOWN, 128, 8, 128), np.float32)
    npr = np.arange(1024)
    ng = (npr + 8 * c) % 1024
    for i in range(NOWN):
        t = 128 * (8 * i + c) + ql
        valid = (ng[:, None] <= 1022) & ((16 * ng[:, None] + 31) <= t[None, :])
        m = np.where(valid, 0.0, NEG).astype(np.float32)
        cm[i] = m.reshape(8, 128, 128).transpose(1, 0, 2)
    T["cmask"] = cm
    nl = np.arange(128)
    alc = np.zeros((4, 2, 128), np.float32)
    alc[0, :, :] = 16 * nl
    alc[1, :, :] = 1
    alc[2, :, :] = 1
    alc[3, 1, :] = ((896 + nl) >= (1024 - 8 * c)).astype(np.float32)
    T["alcl"] = alc
    wm = np.zeros((2, 5, 128, 128), np.float32)
    kl = np.arange(128)
    for st in range(2):
        for off in range(5):
            d = 128 * (4 - off) + ql[None, :] - kl[:, None]
            ok = (d >= 0) & (d < 512)
            if st == 0:
                gt = c - 4 + off
                if gt < 0:
                    ok = np.zeros_like(ok)
            wm[st, off] = np.where(ok, 0.0, NEG)
    T["wm"] = wm.transpose(2, 0, 1, 3).copy()
    pm = np.zeros((2, 4, 128, 128), np.float32)
    ph = np.zeros((2, 4, 32, 128), np.float32)
    for st in range(2):
        first = (st == 0 and c == 0)
        for g, w in enumerate((2, 4, 8, 16)):
            for tt in range(128):
                cnt = min(tt + 1, w) if first else w
                for j in range(tt - w + 1, tt + 1):
                    if j >= 0:
                        pm[st, g, j, tt] += 1.0 / cnt
                    elif not first:
                        ph[st, g, 32 + j, tt] += 1.0 / cnt
                pm[st, g, tt, tt] -= 1.0
    T["pm"] = pm.transpose(2, 0, 1, 3).copy()
    T["ph"] = ph.transpose(2, 0, 1, 3).copy()
    return T


_SHARED = None


def shared_tables():
    global _SHARED
    if _SHARED is not None:
        return _SHARED
    sl = _slopes()
    ql = np.arange(128)
    T = {}
    T["ident"] = np.eye(128, dtype=np.float32)
    ov = np.zeros((1024, 256), np.float32)
    for n in range(1024):
        for j in range(256):
            for wrap in (0, 256):
                o = min(16 * n + 32, 64 * (j + wrap) + 64) - max(16 * n, 64 * (j + wrap))
                if o > 0:
                    ov[n, j] += o / 32.0
    T["ov"] = ov.reshape(8, 128, 256).transpose(1, 0, 2).copy()
    auxl = np.zeros((64, 16, 128), np.float32)
    kl = np.arange(128)
    for kb in range(16):
        for j in range(32):
            auxl[32 + j, kb, :] = (j == 2 * kb + kl // 64)
        auxl[0, kb, :] = 128 * kb
        auxl[1, kb, :] = kl
        auxl[2, kb, :] = 1
        auxl[3, kb, :] = 1
        auxl[4, kb, :] = 1.0 if kb >= 9 else 0.0
    T["auxl"] = auxl
    auxr = np.zeros((NOWN, 2, 5, 8, 4, 128), np.float32)
    alcr = np.zeros((NOWN, 2, 4, 8, 4, 128), np.float32)
    for i in range(NOWN):
        for g in range(2):
            for r in range(4):
                s_ = sl[4 * g + r]
                for ka in range(8):
                    auxr[i, g, 0, ka, r, :] = s_
                    auxr[i, g, 1, ka, r, :] = s_
                    auxr[i, g, 2, ka, r, :] = -s_ * 128 * (8 * i - 16 * ka)
                    auxr[i, g, 3, ka, r, :] = -s_ * ql
                    auxr[i, g, 4, ka, r, :] = -s_ * 16384 if ka == 7 else 0.0
                    alcr[i, g, 0, ka, r, :] = s_
                    alcr[i, g, 1, ka, r, :] = -s_ * (ql - 31)
                    alcr[i, g, 2, ka, r, :] = -s_ * 128 * (8 * i - 16 * ka)
                    alcr[i, g, 3, ka, r, :] = -s_ * 16384 if ka == 7 else 0.0
    T["auxr"] = auxr.reshape(NOWN, 2, 5, 8 * 512)
    T["alcr"] = alcr.reshape(NOWN, 2, 4, 8 * 512)
    _SHARED = T
    return T


def build(cfg):
    own = cfg.get("own", list(range(NOWN)))
    p1 = cfg.get("p1", list(range(NT)))
    do_ffn = cfg.get("ffn", True)
    dbg = cfg.get("dbg", False)
    stop_after = cfg.get("stop_after", 99)

    nc = bass.Bass("TRN2", target_bir_lowering=False)

    def din(name, shape, dt=F32):
        return nc.dram_tensor(name, list(shape), dt, kind="ExternalInput").ap()

    xr = din("xr", [S, D])
    ccol_d = din("ccol", [128, 8])
    wada = din("w_ada", [D, 6 * D])
    bada = din("b_ada", [1, 6 * D])
    g1c_d = din("g1col", [128, 8])
    g2c_d = din("g2col", [128, 8])
    w_in = din("w_in", [D, 1816])
    wpool_d = din("w_pool", [128, 4, 128])
    smallc_d = din("smallc", [128, 16])
    w1k_d = din("cmp_w1_k", [64, 32, 256])
    w1v_d = din("cmp_w1_v", [64, 32, 256])
    w2_d = din("cmp_w2", [128, 2, 2, 64])
    b2_d = din("cmp_b2", [1, 128])
    b1r_d = din("cmp_b1row", [1, 512])
    pos_d = din("cmp_posT", [64, 2, 32])
    w_out = din("w_out", [D, D])
    w_ff1 = din("w_ff1", [D, DFF])
    w_ff2 = din("w_ff2", [DFF, D])
    ident_d = din("ident", [128, 128])
    ov_d = din("ov", [128, 8, 256], BF16)
    auxl_d = din("auxl", [64, 16, 128], BF16)
    auxr_d = din("auxr", [NOWN, 2, 5, 4096], BF16)
    alcr_d = din("alcr", [NOWN, 2, 4, 4096], BF16)
    t1_d = din("t1", [NOWN, 128, 256])
    cmask_d = din("cmask", [NOWN, 128, 8, 128], BF16)
    alcl_d = din("alcl", [4, 2, 128], BF16)
    wm_d = din("wm", [128, 2, 5, 128], BF16)
    pm_d = din("pm", [128, 2, 4, 128])
    ph_d = din("ph", [32, 2, 4, 128])
    out = nc.dram_tensor("out", [NOWN * 128, D], F32, kind="ExternalOutput").ap()
    kw_scr = nc.dram_tensor("kw_scr", [NT, 128, 128], BF16, kind="Internal").ap()
    vw_scr = nc.dram_tensor("vw_scr", [NT, 128, 132], BF16, kind="Internal").ap()
    dbg_out = {}

    def dout(name, shape):
        dbg_out[name] = nc.dram_tensor(name, list(shape), F32, kind="ExternalOutput").ap()
        return dbg_out[name]

    with ExitStack() as es:
      P = Prog(nc, es)
      try:
        return _build_body(nc, P, es, cfg, locals())
      except _Stop:
        return nc, {}, P


def _build_body(nc, P, es, cfg, L):
    globals().update({k: v for k, v in L.items() if k not in ("nc", "P", "es", "cfg")})
    own = L["own"]; p1 = L["p1"]; do_ffn = L["do_ffn"]; dbg = L["dbg"]; stop_after = L["stop_after"]; dbg_out = L["dbg_out"]; dout = L["dout"]
    if True:
        banks = [P.ps("bk%d" % i, [128, 512], F32) for i in range(8)]
        ident = P.sb("ident", [128, 128], F32)
        identb = P.sb("identb", [128, 128], BF16)
        ones = P.sb("ones", [128, 128], F32)
        cols = P.sb("cols", [128, 64], F32)
        modcol = P.sb("modcol", [128, 32], F32)
        GA = P.sb("GA", [128, 2048], F32)
        kw_scr_b = [Buf(kw_scr, "kwscr%d" % s) for s in range(NT)]
        vw_scr_b = [Buf(vw_scr, "vwscr%d" % s) for s in range(NT)]
        out_b = [Buf(out, "out%d" % i) for i in range(NOWN)]

        P.dma("sp", ident[:], ident_d[:, :], writes=[ident])
        P.op("dve", lambda e: e.tensor_copy(out=identb[:], in_=ident[:]), reads=[ident], writes=[identb])
        P.op("pool", lambda e: e.memset(ones[:], 1.0), writes=[ones])
        P.dma("sp", cols[:, 0:8], ccol_d[:, :], writes=[cols])
        P.dma("sp", cols[:, 8:16], g1c_d[:, :], writes=[cols])
        P.dma("sp", cols[:, 16:24], g2c_d[:, :], writes=[cols])
        P.dma("sp", cols[:, 40:56], smallc_d[:, :], writes=[cols])
        P.op("dve", lambda e: e.tensor_scalar(out=cols[:, 56:57], in0=cols[:, 43:44], scalar1=0.125, scalar2=None, op0=ALU.mult),
             reads=[cols], writes=[cols])

        with ExitStack() as e0:
            modrow = P.sb("modrow", [1, 6 * D], F32, e0)
            badar = P.sb("badar", [1, 6 * D], F32, e0)
            wst = [P.sb("wst%d" % j, [128, 8, 512], F32, e0) for j in range(4)]
            P.dma("sp", badar[:], bada[:, :], writes=[badar])
            wada_v = wada.rearrange("(k p) n -> p k n", p=128)
            for n in range(12):
                ws_ = wst[n % 4]
                P.dma("sp", ws_[:], wada_v[:, :, n * 512:(n + 1) * 512], writes=[ws_])
                bk = banks[n % 2]
                for k in range(8):
                    P.op("pe", lambda e, k=k, bk=bk, ws_=ws_: e.matmul(bk[0:1, 0:512], lhsT=cols[:, k:k + 1], rhs=ws_[:, k, :],
                                                                     start=(k == 0), stop=(k == 7)),
                         reads=[cols, ws_], writes=[bk])
                P.op("dve", lambda e, n=n, bk=bk: e.tensor_tensor(out=modrow[0:1, n * 512:(n + 1) * 512], in0=bk[0:1, 0:512],
                                                                  in1=badar[0:1, n * 512:(n + 1) * 512], op=ALU.add),
                     reads=[bk, badar], writes=[modrow])
            bk = banks[2]
            for j in range(32):
                off = [0, 1024, 3072, 4096][j // 8] + (j % 8) * 128
                P.op("pe", lambda e, j=j, off=off: e.matmul(bk[:, j:j + 1], lhsT=modrow[0:1, off:off + 128], rhs=ones[0:1, 0:1],
                                                            start=True, stop=True),
                     reads=[modrow, ones], writes=[bk])
            P.op("dve", lambda e: e.tensor_copy(out=modcol[:], in_=bk[:, 0:32]), reads=[bk], writes=[modcol])
            for h in range(4):
                off = (2048 if h < 2 else 5120) + (h % 2) * 512
                bk2 = banks[3 + h % 2]
                P.op("pe", lambda e, off=off, bk2=bk2: e.matmul(bk2[:, 0:512], lhsT=ones[0:1, 0:128], rhs=modrow[0:1, off:off + 512],
                                                                start=True, stop=True),
                     reads=[modrow, ones], writes=[bk2])
                P.op("act", lambda e, h=h, bk2=bk2: e.activation(out=GA[:, h * 512:(h + 1) * 512], in_=bk2[:, 0:512], func=AF.Copy),
                     reads=[bk2], writes=[GA])
            for (dst, gsrc, scsrc) in ((24, 8, 8), (32, 16, 24)):
                P.op("dve", lambda e, dst=dst, gsrc=gsrc, scsrc=scsrc: e.scalar_tensor_tensor(
                    out=cols[:, dst:dst + 8], in0=modcol[:, scsrc:scsrc + 8], scalar=1.0, in1=cols[:, gsrc:gsrc + 8],
                    op0=ALU.add, op1=ALU.mult), reads=[modcol, cols], writes=[cols])

        P.barrier()

        def prep_win(lo, hi, Wdst, BIASdst, esx, doff=0):
            ncol = hi - lo
            chunks = []
            o = 0
            while o < ncol:
                w_ = min(512, ncol - o)
                chunks.append((o, w_))
                o += w_
            with ExitStack() as el:
                wst2 = [P.sb("wst2_%d" % j, [128, ncol], F32, el) for j in range(3)]
                brow = P.sb("brow", [1, ncol], F32, el)
                for k in range(8):
                    ws_ = wst2[k % 3]
                    P.dma("sp", ws_[:], w_in[k * 128:(k + 1) * 128, lo:hi], writes=[ws_])
                    P.op("dve", lambda e, k=k, ws_=ws_: e.tensor_scalar(out=Wdst[:, k, doff:doff + ncol], in0=ws_[:], scalar1=cols[:, 24 + k:25 + k],
                                                                         scalar2=None, op0=ALU.mult),
                         reads=[ws_, cols], writes=[Wdst])
                    for ci, (o, w_) in enumerate(chunks):
                        bk = banks[4 + ci]
                        P.op("pe", lambda e, k=k, bk=bk, o=o, w_=w_, ws_=ws_: e.matmul(bk[0:1, 0:w_], lhsT=modcol[:, k:k + 1],
                                                                                      rhs=ws_[:, o:o + w_], start=(k == 0), stop=(k == 7)),
                             reads=[modcol, ws_], writes=[bk])
                for ci, (o, w_) in enumerate(chunks):
                    bk = banks[4 + ci]
                    P.op("dve", lambda e, bk=bk, o=o, w_=w_: e.tensor_copy(out=brow[0:1, o:o + w_], in_=bk[0:1, 0:w_]),
                         reads=[bk], writes=[brow])
                for ci, (o, w_) in enumerate(chunks):
                    bk = banks[4 + ci]
                    P.op("pe", lambda e, bk=bk, o=o, w_=w_: e.matmul(bk[:, 0:w_], lhsT=ones[0:1, 0:128], rhs=brow[0:1, o:o + w_],
                                                                    start=True, stop=True), reads=[ones, brow], writes=[bk])
                    P.op("act", lambda e, bk=bk, o=o, w_=w_: e.activation(out=BIASdst[:, doff + o:doff + o + w_], in_=bk[:, 0:w_], func=AF.Copy),
                         reads=[bk], writes=[BIASdst])
            P.barrier()

        with ExitStack() as eatt:
            KselT = P.sb("KselT", [128, S], BF16, eatt)
            Vsel = P.sb("Vsel", [128, NT + 1, 2, 66], BF16, eatt)
            KcmpT = P.sb("KcmpT", [128, 1024], BF16, eatt)
            VOc = P.sb("VOc", [128, 8, 2, 322], BF16, eatt)
            P.op("pool", lambda e: e.memset(Vsel[:, NT, :, :], 0.0), writes=[Vsel])
            P.op("pool", lambda e: e.memset(Vsel[:, :, :, 64:66], 1.0), writes=[Vsel])
            P.op("pool", lambda e: e.memset(VOc[:, :, :, 64:65], 1.0), writes=[VOc])
            for g in range(2):
                P.dma("sp", VOc[:, :, g, 65:321], ov_d[:, :, :], writes=[VOc])

            with ExitStack() as e1:
                Wkv = P.sb("Wkv", [128, 8, 768], BF16, e1)
                BIASkv = P.sb("BIASkv", [128, 768], F32, e1)
                prep_win(1024, 1792, Wkv, BIASkv, e1)
                W1c = [P.sb("W1c%d" % kv, [128, 32, 256], BF16, e1) for kv in range(2)]
                W2c = P.sb("W2c", [128, 2, 2, 64], BF16, e1)
                B2c = P.sb("B2c", [128, 128], F32, e1)
                posT = P.sb("posT", [128, 2, 32], BF16, e1)
                B1bc = P.sb("B1bc", [128, 512], F32, e1)
                cT = [[P.sb("cT%d_%d" % (kv, b), [128, 2176], BF16, e1) for b in range(2)] for kv in range(2)]
                for kv in range(2):
                    for b_ in range(2):
                        P.op("pool", lambda e, kv=kv, b_=b_: e.memset(cT[kv][b_][:], 0.0), writes=[cT[kv][b_]])
                for kv, wd in enumerate((w1k_d, w1v_d)):
                    with ExitStack() as el2:
                        st2 = P.sb("w1st%d" % kv, [128, 32, 256], F32, el2)
                        P.dma("sp", st2[0:64], wd[:, :, :], writes=[st2])
                        P.dma("sp", st2[64:128], wd[:, :, :], writes=[st2])
                        P.op("dve", lambda e, kv=kv, st2=st2: e.tensor_copy(out=W1c[kv][:], in_=st2[:]), reads=[st2], writes=[W1c[kv]])
                    P.barrier()
                with ExitStack() as el:
                    w2st = P.sb("w2st", [128, 2, 2, 64], F32, el)
                    P.dma("sp", w2st[:], w2_d[:, :, :, :], writes=[w2st])
                    P.op("dve", lambda e: e.tensor_copy(out=W2c[:], in_=w2st[:]), reads=[w2st], writes=[W2c])
                    b2row = P.sb("b2row", [1, 128], F32, el)
                    P.dma("sp", b2row[:], b2_d[:, :], writes=[b2row])
                    P.op("pe", lambda e: e.matmul(banks[0][:, 0:128], lhsT=ones[0:1, 0:128], rhs=b2row[0:1, :], start=True, stop=True),
                         reads=[ones, b2row], writes=[banks[0]])
                    P.op("act", lambda e: e.activation(out=B2c[:], in_=banks[0][:, 0:128], func=AF.Copy), reads=[banks[0]], writes=[B2c])
                    posst = P.sb("posst", [128, 2, 32], F32, el)
                    P.dma("sp", posst[0:64], pos_d[:, :, :], writes=[posst])
                    P.dma("sp", posst[64:128], pos_d[:, :, :], writes=[posst])
                    P.op("dve", lambda e: e.tensor_copy(out=posT[:], in_=posst[:]), reads=[posst], writes=[posT])
                    b1row = P.sb("b1row", [1, 512], F32, el)
                    b1eff = P.sb("b1eff", [1, 512], F32, el)
                    P.dma("sp", b1row[:], b1r_d[:, :], writes=[b1row])
                    bk = banks[1]
                    for kv in range(2):
                        for l in range(32):
                            P.op("pe", lambda e, kv=kv, l=l: e.matmul(bk[0:1, kv * 256:(kv + 1) * 256], lhsT=posT[0:64, kv, l:l + 1],
                                                                      rhs=W1c[kv][0:64, l, :], start=(l == 0), stop=(l == 31)),
                                 reads=[W1c[kv], posT], writes=[bk])
                    P.op("dve", lambda e: e.tensor_tensor(out=b1eff[:], in0=bk[0:1, :], in1=b1row[:], op=ALU.add), reads=[bk, b1row], writes=[b1eff])
                    P.op("pe", lambda e: e.matmul(banks[0][:, :], lhsT=ones[0:1, 0:128], rhs=b1eff[0:1, :], start=True, stop=True),
                         reads=[ones, b1eff], writes=[banks[0]])
                    P.op("act", lambda e: e.activation(out=B1bc[:], in_=banks[0][:, :], func=AF.Copy), reads=[banks[0]], writes=[B1bc])
                P.barrier()

                xs = [P.sb("xs%d" % j, [128, D], F32, e1) for j in range(4)]
                xT = [P.sb("xT%d" % j, [128, 8, 128], BF16, e1) for j in range(3)]
                junk = P.sb("junk", [128, D], F32, e1)
                st = [P.sb("st%d" % j, [128, 16], F32, e1) for j in range(3)]
                kvb = [P.sb("kvb%d" % j, [128, 768], F32, e1) for j in range(2)]
                sq = P.sb("sq", [128, 2, 2, 64], F32, e1)
                kn = [P.sb("kn%d" % j, [128, 2, 2, 64], F32, e1) for j in range(2)]
                kwst = [P.sb("kwst%d" % j, [128, 128], BF16, e1) for j in range(2)]
                vwst = [P.sb("vwst%d" % j, [128, 2, 66], BF16, e1) for j in range(2)]
                for j in range(2):
                    P.op("pool", lambda e, j=j: e.memset(vwst[j][:, :, 64:66], 1.0), writes=[vwst[j]])
                hidT = P.sb("hidT", [128, 8, 128], BF16, e1)
                spill0 = P.sb("spill0", [128, 2, 128], BF16, e1)
                hpre = P.sb("hpre", [128, 2, 512], F32, e1)
                hgel = P.sb("hgel", [128, 2, 512], F32, e1)
                ctok = P.sb("ctok", [128, 2, 2, 64], F32, e1)
                ckn = P.sb("ckn", [128, 2, 64], F32, e1)
                cst = P.sb("cst", [128, 8], F32, e1)
                csq = P.sb("csq", [128, 2, 64], F32, e1)

                def compress(j):
                    b = j % 2
                    hb = [banks[6], banks[7]]
                    for g in range(2):
                        bk = hb[g]
                        for kv in range(2):
                            for l in range(32):
                                P.op("pe", lambda e, kv=kv, g=g, l=l, bk=bk: e.matmul(
                                    bk[:, kv * 256:(kv + 1) * 256], lhsT=cT[kv][b][g * 64:(g + 1) * 64, l:l + 16 * 127 + 1:16],
                                    rhs=W1c[kv][g * 64:(g + 1) * 64, l, :], start=(l == 0), stop=(l == 31)),
                                    reads=[W1c[kv], cT[kv][b]], writes=[bk])
                    for g in range(2):
                        bk = hb[g]
                        P.op("dve", lambda e, g=g, bk=bk: e.tensor_tensor(out=hpre[:, g, :], in0=bk[:, :], in1=B1bc[:], op=ALU.add),
                             reads=[bk, B1bc], writes=[hpre])
                        P.op("act", lambda e, g=g: e.activation(out=hgel[:, g, :], in_=hpre[:, g, :], func=AF.Gelu_apprx_tanh),
                             reads=[hpre], writes=[hgel])
                    hid5 = hidT[:].rearrange("p (kv g ch) n -> p kv g ch n", kv=2, g=2)
                    for g in range(2):
                        bk = hb[g]
                        for t4 in range(4):
                            P.op("pe", lambda e, g=g, t4=t4, bk=bk: e.transpose(out=bk[:, t4 * 128:(t4 + 1) * 128], in_=hgel[:, g, t4 * 128:(t4 + 1) * 128],
                                                                                identity=ident[:]), reads=[hgel, ident], writes=[bk])
                        src = bk[:, :].rearrange("p (kv ch n) -> p kv ch n", kv=2, ch=2)
                        if g == 0:
                            P.op("act", lambda e, src=src: e.activation(out=hid5[:, :, 0, :, :], in_=src, func=AF.Copy), reads=[bk], writes=[hidT])
                        else:
                            P.op("dve", lambda e, src=src: e.tensor_copy(out=hid5[:, :, 1, :, :], in_=src), reads=[bk], writes=[hidT])
                    bk = banks[4]
                    for kv in range(2):
                        for g in range(2):
                            o = (kv * 2 + g) * 64
                            for ch in range(2):
                                idx = (kv * 2 + g) * 2 + ch
                                P.op("pe", lambda e, kv=kv, ch=ch, idx=idx, o=o: e.matmul(
                                    bk[:, o:o + 64], lhsT=hidT[:, idx, :], rhs=W2c[:, kv, ch, :], start=(ch == 0), stop=(ch == 1)),
                                    reads=[hidT, W2c], writes=[bk])
                    for kv in range(2):
                        P.op("dve", lambda e, kv=kv: e.tensor_tensor(
                            out=ctok[:, kv, :, :], in0=bk[:, kv * 128:(kv + 1) * 128].rearrange("p (g d) -> p g d", g=2),
                            in1=B2c[:, kv * 64:(kv + 1) * 64].unsqueeze(1).broadcast_to([128, 2, 64]), op=ALU.add),
                            reads=[bk, B2c], writes=[ctok])
                    P.op("dve", lambda e: e.tensor_tensor(out=csq[:], in0=ctok[:, 0, :, :], in1=ctok[:, 0, :, :], op=ALU.mult),
                         reads=[ctok], writes=[csq])
                    P.op("dve", lambda e: e.tensor_reduce(out=cst[:, 0:2], in_=csq[:], axis=AX.X, op=ALU.add), reads=[csq], writes=[cst])
                    P.op("dve", lambda e: e.tensor_scalar(out=cst[:, 2:4], in0=cst[:, 0:2], scalar1=1.0 / 64, scalar2=EPS,
                                                           op0=ALU.mult, op1=ALU.add), reads=[cst], writes=[cst])
                    P.op("act", lambda e: e.activation(out=cst[:, 4:6], in_=cst[:, 2:4], func=AF.Sqrt), reads=[cst], writes=[cst])
                    P.op("dve", lambda e: e.reciprocal(out=cst[:, 6:8], in_=cst[:, 4:6]), reads=[cst], writes=[cst])
                    P.op("dve", lambda e: e.tensor_tensor(out=ckn[:], in0=ctok[:, 0, :, :],
                                                           in1=cst[:, 6:8].unsqueeze(2).broadcast_to([128, 2, 64]), op=ALU.mult),
                         reads=[ctok, cst], writes=[ckn])
                    P.op("pe", lambda e: e.transpose(out=bk[:, 256:384], in_=ckn[:].rearrange("p g d -> p (g d)"), identity=ident[:]),
                         reads=[ckn, ident], writes=[bk])
                    P.op("act", lambda e: e.activation(out=KcmpT[:, j * 128:(j + 1) * 128], in_=bk[:, 256:384], func=AF.Copy,
                                                        scale=cols[:, 42:43]), reads=[bk, cols], writes=[KcmpT])
                    P.op("pool", lambda e: e.tensor_copy(out=VOc[:, j, :, 0:64], in_=ctok[:, 1, :, :]), reads=[ctok], writes=[VOc])

                nsl = len(p1)

                def loadx(si):
                    s = p1[si]
                    P.dma("sp", xs[si % 4][:], xr[s * 128:(s + 1) * 128, :], writes=[xs[si % 4]])

                def stageA1(si):
                    s = p1[si]
                    x_ = xs[si % 4]
                    xT_ = xT[si % 3]
                    st_ = st[si % 3]
                    kv_ = kvb[si % 2]
                    kn_ = kn[si % 2]
                    P.op("act", lambda e, x_=x_, st_=st_: e.activation(out=junk[:], in_=x_[:], func=AF.Square, accum_out=st_[:, 0:1]),
                         reads=[x_], writes=[junk, st_])
                    for k in range(8):
                        bk = banks[k // 4]
                        P.op("pe", lambda e, k=k, bk=bk, x_=x_: e.transpose(out=bk[:, (k % 4) * 128:(k % 4 + 1) * 128],
                                                                           in_=x_[:, k * 128:(k + 1) * 128], identity=ident[:]),
                             reads=[x_, ident], writes=[bk])
                    P.op("act", lambda e, xT_=xT_: e.activation(out=xT_[:, 0:4, :].rearrange("p a b -> p (a b)"), in_=banks[0][:, :], func=AF.Copy),
                         reads=[banks[0]], writes=[xT_])
                    P.op("dve", lambda e, xT_=xT_: e.tensor_copy(out=xT_[:, 4:8, :].rearrange("p a b -> p (a b)"), in_=banks[1][:, :]),
                         reads=[banks[1]], writes=[xT_])
                    P.op("dve", lambda e, st_=st_: e.tensor_scalar(out=st_[:, 1:2], in0=st_[:, 0:1], scalar1=1.0 / D, scalar2=EPS,
                                                                    op0=ALU.mult, op1=ALU.add), reads=[st_], writes=[st_])
                    P.op("act", lambda e, st_=st_: e.activation(out=st_[:, 2:3], in_=st_[:, 1:2], func=AF.Sqrt), reads=[st_], writes=[st_])
                    P.op("dve", lambda e, st_=st_: e.reciprocal(out=st_[:, 3:4], in_=st_[:, 2:3]), reads=[st_], writes=[st_])

                def stageA2(si):
                    s = p1[si]
                    x_ = xs[si % 4]
                    xT_ = xT[si % 3]
                    st_ = st[si % 3]
                    kv_ = kvb[si % 2]
                    kn_ = kn[si % 2]
                    for half in range(2):
                        bk = banks[2 + half]
                        for k in range(8):
                            P.op("pe", lambda e, k=k, bk=bk, half=half, xT_=xT_: e.matmul(
                                bk[:, 0:384], lhsT=xT_[:, k, :], rhs=Wkv[:, k, half * 384:(half + 1) * 384], start=(k == 0), stop=(k == 7)),
                                reads=[xT_, Wkv], writes=[bk])
                        P.op("dve", lambda e, bk=bk, half=half, kv_=kv_, st_=st_: e.scalar_tensor_tensor(
                            out=kv_[:, half * 384:(half + 1) * 384], in0=bk[:, 0:384], scalar=st_[:, 3:4],
                            in1=BIASkv[:, half * 384:(half + 1) * 384], op0=ALU.mult, op1=ALU.add),
                            reads=[bk, st_, BIASkv], writes=[kv_])
                    ksel_v = kv_[:, 256:768].rearrange("p (a x) -> p a x", a=2)[:, :, 0:128].rearrange("p a (h d) -> p a h d", d=64)
                    P.op("act", lambda e, ksel_v=ksel_v: e.activation(out=sq[:], in_=ksel_v, func=AF.Square), reads=[kv_], writes=[sq])
                    P.op("dve", lambda e, st_=st_: e.tensor_reduce(out=st_[:, 4:8], in_=sq[:].rearrange("p a h d -> p (a h) d"),
                                                                    axis=AX.X, op=ALU.add), reads=[sq], writes=[st_])
                    P.op("dve", lambda e, st_=st_: e.tensor_scalar(out=st_[:, 8:12], in0=st_[:, 4:8], scalar1=1.0 / 64, scalar2=EPS,
                                                                    op0=ALU.mult, op1=ALU.add), reads=[st_], writes=[st_])
                    P.op("act", lambda e, st_=st_: e.activation(out=st_[:, 4:8], in_=st_[:, 8:12], func=AF.Sqrt), reads=[st_], writes=[st_])
                    P.op("dve", lambda e, st_=st_: e.reciprocal(out=st_[:, 12:16], in_=st_[:, 4:8]), reads=[st_], writes=[st_])
                    P.op("dve", lambda e, ksel_v=ksel_v, kn_=kn_, st_=st_: e.tensor_tensor(
                        out=kn_[:], in0=ksel_v,
                        in1=st_[:, 12:16].rearrange("p (a h) -> p a h", a=2).unsqueeze(3).broadcast_to([128, 2, 2, 64]), op=ALU.mult),
                        reads=[kv_, st_], writes=[kn_])

                def stageB(si):
                    s = p1[si]
                    x_ = xs[si % 4]
                    xT_ = xT[si % 3]
                    st_ = st[si % 3]
                    kv_ = kvb[si % 2]
                    kn_ = kn[si % 2]
                    bk = banks[4]
                    P.op("pe", lambda e, kn_=kn_: e.transpose(out=bk[:, 0:128], in_=kn_[:, 0, :, :].rearrange("p h d -> p (h d)"),
                                                               identity=ident[:]), reads=[kn_, ident], writes=[bk])
                    bk5 = banks[5]
                    P.op("pe", lambda e, kv_=kv_: e.transpose(out=bk[:, 256:384], in_=kv_[:, 0:128], identity=ident[:]),
                         reads=[kv_, ident], writes=[bk])
                    P.op("pe", lambda e, kn_=kn_: e.transpose(out=bk5[:, 128:256], in_=kn_[:, 1, :, :].rearrange("p h d -> p (h d)"),
                                                               identity=ident[:]), reads=[kn_, ident], writes=[bk5])
                    P.op("pe", lambda e, kv_=kv_: e.transpose(out=bk5[:, 384:512], in_=kv_[:, 128:256], identity=ident[:]),
                         reads=[kv_, ident], writes=[bk5])
                    P.op("act", lambda e, s=s: e.activation(out=KselT[:, s * 128:(s + 1) * 128], in_=bk[:, 0:128], func=AF.Copy,
                                                             scale=cols[:, 40:41]), reads=[bk, cols], writes=[KselT])
                    kw_ = kwst[si % 2]
                    vw_ = vwst[si % 2]
                    P.op("dve", lambda e, kw_=kw_: e.tensor_scalar(out=kw_[:], in0=bk5[:, 128:256], scalar1=cols[:, 41:42], scalar2=None,
                                                                    op0=ALU.mult), reads=[bk5, cols], writes=[kw_])
                    cj = s // 16
                    cp = (s % 16) * 128
                    P.op("act", lambda e, cj=cj, cp=cp: e.activation(out=cT[0][cj % 2][:, cp:cp + 128], in_=bk[:, 256:384], func=AF.Copy),
                         reads=[bk], writes=[cT[0][cj % 2]])
                    P.op("dve", lambda e, cj=cj, cp=cp: e.tensor_copy(out=cT[1][cj % 2][:, cp:cp + 128], in_=bk5[:, 384:512]),
                         reads=[bk5], writes=[cT[1][cj % 2]])
                    if s == 0:
                        P.op("act", lambda e: e.activation(out=spill0[:, 0, :], in_=bk[:, 256:384], func=AF.Copy), reads=[bk], writes=[spill0])
                        P.op("dve", lambda e: e.tensor_copy(out=spill0[:, 1, :], in_=bk5[:, 384:512]), reads=[bk5], writes=[spill0])
                    if s % 16 == 0 and s > 0:
                        P.op("act", lambda e, cj=cj: e.activation(out=cT[0][(cj - 1) % 2][:, 2048:2176], in_=bk[:, 256:384], func=AF.Copy),
                             reads=[bk], writes=[cT[0][(cj - 1) % 2]])
                        P.op("dve", lambda e, cj=cj: e.tensor_copy(out=cT[1][(cj - 1) % 2][:, 2048:2176], in_=bk5[:, 384:512]),
                             reads=[bk5], writes=[cT[1][(cj - 1) % 2]])
                    P.op("act", lambda e, s=s, kv_=kv_: e.activation(out=Vsel[:, s, :, 0:64],
                                                                       in_=kv_[:, 384:512].rearrange("p (g d) -> p g d", g=2), func=AF.Copy),
                         reads=[kv_], writes=[Vsel])
                    P.op("act", lambda e, vw_=vw_, kv_=kv_: e.activation(out=vw_[:, :, 0:64],
                                                                           in_=kv_[:, 640:768].rearrange("p (g d) -> p g d", g=2), func=AF.Copy),
                         reads=[kv_], writes=[vw_])
                    P.dma("pool", kw_scr[s, :, :], kw_[:], reads=[kw_], writes=[kw_scr_b[s]], owner=kw_)
                    P.dma("pool", vw_scr[s, :, :], vw_[:].rearrange("p g d -> p (g d)"), reads=[vw_], writes=[vw_scr_b[s]], owner=vw_)
                    if s % 16 == 0 and s > 0 and (s - 16) in p1:
                        compress(s // 16 - 1)
                    if s == 127:
                        for kv in range(2):
                            P.op("pool", lambda e, kv=kv: e.tensor_copy(out=cT[kv][1][:, 2048:2176], in_=spill0[:, kv, :]),
                                 reads=[spill0], writes=[cT[kv][1]])
                        compress(7)

                for j in range(min(3, nsl)):
                    loadx(j)
                stageA1(0)
                if nsl > 1:
                    stageA1(1)
                stageA2(0)
                for si in range(nsl):
                    if si + 3 < nsl:
                        loadx(si + 3)
                    if si + 2 < nsl:
                        stageA1(si + 2)
                    if si + 1 < nsl:
                        stageA2(si + 1)
                    stageB(si)
                if dbg:
                    with ExitStack() as el:
                        dstg = P.sb("dstg", [128, 1024], F32, el)

                        def dump(name, buf, ap2d, ncol):
                            d_ = dout(name, [128, ncol])
                            for o in range(0, ncol, 1024):
                                w_ = min(1024, ncol - o)
                                P.op("dve", lambda e, o=o, w_=w_: e.tensor_copy(out=dstg[:, 0:w_], in_=ap2d[:, o:o + w_]),
                                     reads=[buf], writes=[dstg])
                                P.dma("sp", d_[:, o:o + w_], dstg[:, 0:w_], reads=[dstg])
                        dump("d_kselT", KselT, KselT[:], S)
                        dump("d_vsel", Vsel, Vsel[:, 0:NT].rearrange("p a g d -> p (a g d)"), NT * 132)
                        dump("d_kcmpT", KcmpT, KcmpT[:], 1024)
                        dump("d_voc", VOc, VOc[:].rearrange("p a g d -> p (a g d)"), 8 * 2 * 322)
                        P.finish("sp", [dstg])
                    P.barrier()
            if stop_after <= 1:
                P.finish("sp", kw_scr_b + vw_scr_b)
                return nc, dbg_out, P
            P.barrier()
            qt_scr = nc.dram_tensor("qt_scr", [NOWN, 128, 1024], BF16, kind="Internal").ap()
            g_scr = nc.dram_tensor("g_scr", [NOWN, 128, 24], F32, kind="Internal").ap()
            mp_scr = nc.dram_tensor("mp_scr", [NOWN, 128, 512], BF16, kind="Internal").ap()
            qt_b = [Buf(qt_scr, "qts%d" % i) for i in range(NOWN)]
            g_b = [Buf(g_scr, "gs%d" % i) for i in range(NOWN)]
            mp_b = [Buf(mp_scr, "mps%d" % i) for i in range(NOWN)]
            with ExitStack() as e15:
                Wown = P.sb("Wown", [128, 8, 1048], BF16, e15)
                BIASown = P.sb("BIASown", [128, 1048], F32, e15)
                prep_win(0, 1024, Wown, BIASown, e15, 0)
                prep_win(1792, 1816, Wown, BIASown, e15, 1024)
                Wpoolb = P.sb("Wpoolb", [128, 4, 128], BF16, e15)
                PMf = P.sb("PMf", [128, 2, 4, 128], F32, e15)
                PHf = P.sb("PHf", [32, 2, 4, 128], F32, e15)
                with ExitStack() as el:
                    wp_st = P.sb("wp_st", [128, 4, 128], F32, el)
                    P.dma("sp", wp_st[:], wpool_d[:, :, :], writes=[wp_st])
                    P.op("dve", lambda e: e.tensor_copy(out=Wpoolb[:], in_=wp_st[:]), reads=[wp_st], writes=[Wpoolb])
                    P.dma("sp", PMf[:], pm_d[:, :, :, :], writes=[PMf])
                    P.dma("sp", PHf[:], ph_d[:, :, :, :], writes=[PHf])
                P.barrier()
                NB = 3
                fxo = [P.sb("fxo%d" % j, [128, D], F32, e15) for j in range(NB)]
                fxh = [P.sb("fxh%d" % j, [32, D], F32, e15) for j in range(NB)]
                fxT = [P.sb("fxT%d" % j, [128, 8, 128], BF16, e15) for j in range(NB)]
                fxhT = [P.sb("fxhT%d" % j, [128, 8, 32], BF16, e15) for j in range(NB)]
                fst = [P.sb("fst%d" % j, [128, 48], F32, e15) for j in range(NB)]
                fuq = [P.sb("fuq%d" % j, [128, 1048], F32, e15) for j in range(NB)]
                fuh = [P.sb("fuh%d" % j, [32, 512], F32, e15) for j in range(NB)]
                fjunk = P.sb("fjunk", [128, D], F32, e15)
                fsq = P.sb("fsq", [128, 512], F32, e15)
                fqn = P.sb("fqn", [128, 512], F32, e15)
                fpool = P.sb("fpool", [128, 512], BF16, e15)
                fgt = [P.sb("fgt%d" % j, [128, 24], F32, e15) for j in range(NB)]
                fmp = [P.sb("fmp%d" % j, [128, 4, 128], BF16, e15) for j in range(NB)]
                fqt = [P.sb("fqt%d" % j, [128, 2, 512], BF16, e15) for j in range(NB)]
                for j in range(NB):
                    P.op("pool", lambda e, j=j: e.memset(fqt[j][:], 0.0), writes=[fqt[j]])

                def f_s1(ii):
                    i = own[ii]
                    so = 8 * i
                    hs = (so - 1) % NT
                    xo_, xh_, xT_, xhT_, st_ = fxo[ii % NB], fxh[ii % NB], fxT[ii % NB], fxhT[ii % NB], fst[ii % NB]
                    P.dma("sp", xo_[:], xr[so * 128:(so + 1) * 128, :], writes=[xo_])
                    P.dma("sp", xh_[:], xr[hs * 128 + 96:hs * 128 + 128, :], writes=[xh_])
                    P.op("act", lambda e: e.activation(out=fjunk[:], in_=xo_[:], func=AF.Square, accum_out=st_[:, 0:1]),
                         reads=[xo_], writes=[fjunk, st_])
                    P.op("act", lambda e: e.activation(out=fjunk[0:32, :], in_=xh_[:], func=AF.Square, accum_out=st_[0:32, 4:5]),
                         reads=[xh_], writes=[fjunk, st_])
                    for (a, b_) in ((0, 128), (4, 32)):
                        P.op("dve", lambda e, a=a, b_=b_: e.tensor_scalar(out=st_[0:b_, a + 1:a + 2], in0=st_[0:b_, a:a + 1], scalar1=1.0 / D,
                                                                          scalar2=EPS, op0=ALU.mult, op1=ALU.add), reads=[st_], writes=[st_])
                        P.op("act", lambda e, a=a, b_=b_: e.activation(out=st_[0:b_, a + 2:a + 3], in_=st_[0:b_, a + 1:a + 2], func=AF.Sqrt),
                             reads=[st_], writes=[st_])
                        P.op("dve", lambda e, a=a, b_=b_: e.reciprocal(out=st_[0:b_, a + 3:a + 4], in_=st_[0:b_, a + 2:a + 3]),
                             reads=[st_], writes=[st_])
                    for k in range(8):
                        bk = banks[k // 4]
                        P.op("pe", lambda e, k=k, bk=bk: e.transpose(out=bk[:, (k % 4) * 128:(k % 4 + 1) * 128],
                                                                     in_=xo_[:, k * 128:(k + 1) * 128], identity=ident[:]),
                             reads=[xo_, ident], writes=[bk])
                    P.op("act", lambda e: e.activation(out=xT_[:, 0:4, :].rearrange("p a b -> p (a b)"), in_=banks[0][:, :], func=AF.Copy),
                         reads=[banks[0]], writes=[xT_])
                    P.op("dve", lambda e: e.tensor_copy(out=xT_[:, 4:8, :].rearrange("p a b -> p (a b)"), in_=banks[1][:, :]),
                         reads=[banks[1]], writes=[xT_])
                    for k in range(8):
                        P.op("pe", lambda e, k=k: e.transpose(out=banks[2][:, k * 32:(k + 1) * 32], in_=xh_[:, k * 128:(k + 1) * 128],
                                                               identity=ident[0:32, 0:32]), reads=[xh_, ident], writes=[banks[2]])
                    P.op("act", lambda e: e.activation(out=xhT_[:].rearrange("p a b -> p (a b)"), in_=banks[2][:, 0:256], func=AF.Copy),
                         reads=[banks[2]], writes=[xhT_])

                def f_s2a(ii):
                    i = own[ii]
                    pset = 0 if i == 0 else 1
                    xT_, xhT_, st_, uq, uh = fxT[ii % NB], fxhT[ii % NB], fst[ii % NB], fuq[ii % NB], fuh[ii % NB]
                    gt_, mp_, qt_ = fgt[ii % NB], fmp[ii % NB], fqt[ii % NB]
                    for (bi, lo, w_) in ((3, 0, 512), (4, 512, 512), (5, 1024, 24)):
                        bk = banks[bi]
                        for k in range(8):
                            P.op("pe", lambda e, k=k, bk=bk, lo=lo, w_=w_: e.matmul(bk[:, 0:w_], lhsT=xT_[:, k, :], rhs=Wown[:, k, lo:lo + w_],
                                                                                   start=(k == 0), stop=(k == 7)), reads=[xT_, Wown], writes=[bk])
                        P.op("dve", lambda e, bk=bk, lo=lo, w_=w_: e.scalar_tensor_tensor(
                            out=uq[:, lo:lo + w_], in0=bk[:, 0:w_], scalar=st_[:, 3:4], in1=BIASown[:, lo:lo + w_], op0=ALU.mult, op1=ALU.add),
                            reads=[bk, st_, BIASown], writes=[uq])
                    for k in range(8):
                        P.op("pe", lambda e, k=k: e.matmul(banks[6][0:32, 0:512], lhsT=xhT_[:, k, :], rhs=Wown[:, k, 0:512],
                                                           start=(k == 0), stop=(k == 7)), reads=[xhT_, Wown], writes=[banks[6]])
                    P.op("dve", lambda e: e.scalar_tensor_tensor(out=uh[:], in0=banks[6][0:32, 0:512], scalar=st_[0:32, 7:8],
                                                                 in1=BIASown[0:32, 0:512], op0=ALU.mult, op1=ALU.add),
                         reads=[banks[6], st_, BIASown], writes=[uh])
                    P.op("act", lambda e: e.activation(out=gt_[:], in_=uq[:, 1024:1048], func=AF.Sigmoid), reads=[uq], writes=[gt_])

                def f_s2b(ii):
                    i = own[ii]
                    pset = 0 if i == 0 else 1
                    xT_, xhT_, st_, uq, uh = fxT[ii % NB], fxhT[ii % NB], fst[ii % NB], fuq[ii % NB], fuh[ii % NB]
                    gt_, mp_, qt_ = fgt[ii % NB], fmp[ii % NB], fqt[ii % NB]
                    for g in range(4):
                        P.op("pe", lambda e, g=g: e.matmul(banks[7][:, g * 128:(g + 1) * 128], lhsT=uq[:, g * 128:(g + 1) * 128],
                                                           rhs=PMf[:, pset, g, :], start=True, stop=False), reads=[uq, PMf], writes=[banks[7]])
                        P.op("pe", lambda e, g=g: e.matmul(banks[7][:, g * 128:(g + 1) * 128], lhsT=uh[0:32, g * 128:(g + 1) * 128],
                                                           rhs=PHf[0:32, pset, g, :], start=False, stop=True), reads=[uh, PHf], writes=[banks[7]])
                    P.op("act", lambda e: e.activation(out=fpool[:], in_=banks[7][:, :], func=AF.Copy), reads=[banks[7]], writes=[fpool])
                    for g in range(4):
                        P.op("pe", lambda e, g=g: e.matmul(banks[3][:, g * 128:(g + 1) * 128], lhsT=Wpoolb[:, g, :],
                                                           rhs=fpool[:, g * 128:(g + 1) * 128], start=True, stop=True),
                             reads=[Wpoolb, fpool], writes=[banks[3]])
                    for g in range(4):
                        P.op("dve", lambda e, g=g: e.tensor_scalar(out=mp_[:, g, :], in0=banks[3][:, g * 128:(g + 1) * 128],
                                                                    scalar1=cols[:, 44 + g:45 + g], scalar2=None, op0=ALU.mult),
                             reads=[banks[3], cols], writes=[mp_])
                    P.op("act", lambda e: e.activation(out=fsq[:], in_=uq[:, 512:1024], func=AF.Square), reads=[uq], writes=[fsq])
                    P.op("dve", lambda e: e.tensor_reduce(out=st_[:, 8:16], in_=fsq[:].rearrange("p (h d) -> p h d", d=64), axis=AX.X, op=ALU.add),
                         reads=[fsq], writes=[st_])
                    P.op("dve", lambda e: e.tensor_scalar(out=st_[:, 16:24], in0=st_[:, 8:16], scalar1=1.0 / 64, scalar2=EPS,
                                                           op0=ALU.mult, op1=ALU.add), reads=[st_], writes=[st_])
                    P.op("act", lambda e: e.activation(out=st_[:, 24:32], in_=st_[:, 16:24], func=AF.Sqrt), reads=[st_], writes=[st_])
                    P.op("dve", lambda e: e.reciprocal(out=st_[:, 32:40], in_=st_[:, 24:32]), reads=[st_], writes=[st_])
                    P.op("dve", lambda e: e.tensor_tensor(
                        out=fqn[:].rearrange("p (r g d) -> p g r d", g=2, d=64),
                        in0=uq[:, 512:1024].rearrange("p (g r d) -> p g r d", g=2, d=64),
                        in1=st_[:, 32:40].rearrange("p (g r) -> p g r", g=2).unsqueeze(3).broadcast_to([128, 2, 4, 64]), op=ALU.mult),
                        reads=[uq, st_], writes=[fqn])
                    for r in range(4):
                        P.op("pe", lambda e, r=r: e.transpose(out=banks[4][:, r * 128:(r + 1) * 128], in_=fqn[:, r * 128:(r + 1) * 128],
                                                               identity=ident[:]), reads=[fqn, ident], writes=[banks[4]])
                    for g in range(2):
                        P.op("act", lambda e, g=g: e.activation(out=qt_[g * 64:(g + 1) * 64, g, :], in_=banks[4][g * 64:(g + 1) * 64, :], func=AF.Copy,
                                                                scale=cols[g * 64:(g + 1) * 64, 56:57]), reads=[banks[4], cols], writes=[qt_])
                    P.dma("pool", qt_scr[i, :, :], qt_[:].rearrange("p g q -> p (g q)"), reads=[qt_], writes=[qt_b[i]], owner=qt_)
                    P.dma("pool", g_scr[i, :, :], gt_[:], reads=[gt_], writes=[g_b[i]], owner=gt_)
                    P.dma("pool", mp_scr[i, :, :], mp_[:].rearrange("p g t -> p (g t)"), reads=[mp_], writes=[mp_b[i]], owner=mp_)

                no = len(own)
                if no > 0:
                    f_s1(0)
                if no > 1:
                    f_s1(1)
                if no > 0:
                    f_s2a(0)
                for ii in range(no):
                    if ii + 2 < no:
                        f_s1(ii + 2)
                    if ii + 1 < no:
                        f_s2a(ii + 1)
                    f_s2b(ii)
            P.barrier()

            with ExitStack() as e2:
                Woutb = P.sb("Woutb", [128, 8, 1024], BF16, e2)
                AUXL = P.sb("AUXL", [128, 16, 128], BF16, e2)
                ALCL = P.sb("ALCL", [128, 2, 128], BF16, e2)
                TRI = P.sb("TRI", [128, 128], BF16, e2)
                with ExitStack() as el:
                    wo_st = [P.sb("wo_st%d" % j, [128, 1024], F32, el) for j in range(4)]
                    for k in range(8):
                        ws_ = wo_st[k % 4]
                        P.dma("sp", ws_[:], w_out[k * 128:(k + 1) * 128, :], writes=[ws_])
                        P.op("dve", lambda e, k=k, ws_=ws_: e.tensor_tensor(out=Woutb[:, k, :], in0=ws_[:], in1=GA[:, 0:1024], op=ALU.mult),
                             reads=[ws_, GA], writes=[Woutb])
                    P.op("pool", lambda e: e.memset(AUXL[:], 0.0), writes=[AUXL])
                    P.dma("sp", AUXL[0:64], auxl_d[:, :, :], writes=[AUXL])
                    P.op("pool", lambda e: e.memset(ALCL[:], 0.0), writes=[ALCL])
                    P.dma("sp", ALCL[64:68], alcl_d[:, :, :], writes=[ALCL])
                    P.dma("sp", TRI[:], wm_d[:, 1, 4, :], writes=[TRI])
                P.barrier()

                QTz2 = [P.sb("QTz%d" % p, [128, 2, 512], BF16, e2) for p in range(2)]
                gates2 = [P.sb("gates%d" % p, [128, 24], F32, e2) for p in range(2)]
                mixT2 = [P.sb("mixT%d" % p, [128, 8, 128], BF16, e2) for p in range(2)]
                cmb2 = [P.sb("cmb%d" % p, [128, 8, 128], BF16, e2) for p in range(2)]
                AUXR2 = [[P.sb("AUXR%d_%d" % (p, g), [128, 4096], BF16, e2) for g in range(2)] for p in range(2)]
                for p in range(2):
                    for g in range(2):
                        P.op("pool", lambda e, p=p, g=g: e.memset(AUXR2[p][g][:], 0.0), writes=[AUXR2[p][g]])
                attn2 = [P.sb("attn%d" % p, [128, 8, 64], F32, e2) for p in range(2)]
                kwT2 = [P.sb("kwT%d" % p, [128, 5, 128], BF16, e2) for p in range(2)]
                vwb2 = [P.sb("vwb%d" % p, [128, 6, 132], BF16, e2) for p in range(2)]
                for p in range(2):
                    P.op("pool", lambda e, p=p: e.memset(vwb2[p][:, 5, :], 0.0), writes=[vwb2[p]])
                WMt2 = [P.sb("WMt%d" % p, [128, 5, 128], BF16, e2) for p in range(2)]
                t1t = P.sb("t1t", [128, 256], F32, e2)
                PTc = P.sb("PTc", [128, 8, 512], BF16, e2)
                PTb = [P.sb("PTb%d" % j, [128, 512], BF16, e2) for j in range(4)]
                imp = P.sb("imp", [128, 256], F32, e2)
                vv = P.sb("vv", [128, 256], F32, e2)
                vv2 = P.sb("vv2", [128, 256], F32, e2)
                mb2 = P.sb("mb2", [128, 512], F32, e2)
                m8 = P.sb("m8", [128, 24], F32, e2)
                rsc = P.sb("rsc", [128, 8], F32, e2)
                rsb = P.sb("rsb", [128, 16], F32, e2)
                osb = P.sb("osb", [65, 512], F32, e2)
                otmp = P.sb("otmp", [128, 4, 64], F32, e2)
                xo = P.sb("xo", [128, D], F32, e2)

                def front(ii):
                    i = own[ii]
                    p = ii % 2
                    so = 8 * i
                    pset = 0 if i == 0 else 1
                    QTz_, gates, mixT, cmb, AUXR, attn, kwT_, vwb_, WMt = QTz2[p], gates2[p], mixT2[p], cmb2[p], AUXR2[p], attn2[p], kwT2[p], vwb2[p], WMt2[p]
                    P.dma("sp", QTz_[:].rearrange("p g q -> p (g q)"), qt_scr[i, :, :], reads=[qt_b[i]], writes=[QTz_])
                    P.dma("sp", gates[:], g_scr[i, :, :], reads=[g_b[i]], writes=[gates])
                    P.dma("sp", mixT[:, 0:4, :].rearrange("p g t -> p (g t)"), mp_scr[i, :, :], reads=[mp_b[i]], writes=[mixT])
                    P.dma("sp", t1t[:], t1_d[i, :, :], writes=[t1t])
                    chunks = cmp_chunks(i)
                    for cc in chunks:
                        P.dma("sp", cmb[:, cc, :], cmask_d[i, :, cc, :], writes=[cmb], nowaw=True)
                    kas = sorted(set(s_ // 16 for s_ in sel_slots(i)))
                    for g in range(2):
                        for ka in kas:
                            P.dma("sp", AUXR[g][0:5, ka * 512:(ka + 1) * 512], auxr_d[i, g, :, ka * 512:(ka + 1) * 512], writes=[AUXR[g]], nowaw=True)
                        for cc in chunks:
                            P.dma("sp", AUXR[g][64:68, cc * 512:(cc + 1) * 512], alcr_d[i, g, :, cc * 512:(cc + 1) * 512], writes=[AUXR[g]], nowaw=True)
                    for off in range(5):
                        ws = (so - 4 + off) % NT
                        P.dma("sp", kwT_[:, off, :], kw_scr[ws, :, :], reads=[kw_scr_b[ws]], writes=[kwT_], nowaw=True)
                        P.dma("sp", vwb_[:, off, :], vw_scr[ws, :, :], reads=[vw_scr_b[ws]], writes=[vwb_], nowaw=True)
                    P.dma("sp", WMt[:], wm_d[:, pset, :, :], writes=[WMt])
                    yield
                    nch = len(chunks)
                    for g in range(2):
                        bk = banks[6]
                        for ci, cc in enumerate(chunks):
                            P.op("pe", lambda e: e.matmul(bk[:, :], lhsT=KcmpT[:, cc * 128:(cc + 1) * 128], rhs=QTz_[:, g, :],
                                                          start=True, stop=False), reads=[KcmpT, QTz_], writes=[bk])
                            P.op("pe", lambda e: e.matmul(bk[:, :], lhsT=ALCL[:, 1 if cc == 7 else 0, :],
                                                          rhs=AUXR[g][:, cc * 512:(cc + 1) * 512], start=False, stop=False),
                                 reads=[ALCL, AUXR[g]], writes=[bk])
                            for r in range(4):
                                P.op("pe", lambda e, r=r: e.matmul(bk[:, r * 128:(r + 1) * 128], lhsT=identb[:], rhs=cmb[:, cc, :],
                                                                   start=False, stop=(r == 3)), reads=[identb, cmb], writes=[bk])
                            P.op("act", lambda e: e.activation(out=PTc[:, ci, :], in_=bk[:, :], func=AF.Exp), reads=[bk], writes=[PTc])
                            yield
                        bk = banks[7]
                        for r in range(4):
                            h = g * 4 + r
                            for ci, cc in enumerate(chunks):
                                P.op("pe", lambda e: e.matmul(bk[:, 0:321], lhsT=PTc[:, ci, r * 128:(r + 1) * 128], rhs=VOc[:, cc, g, 0:321],
                                                              start=(ci == 0), stop=(ci == nch - 1)), reads=[PTc, VOc], writes=[bk])
                            P.op("dve", lambda e: e.tensor_scalar(out=rsc[:, r:r + 1], in0=bk[:, 64:65], scalar1=1e-30, scalar2=None,
                                                                  op0=ALU.max), reads=[bk], writes=[rsc])
                            P.op("dve", lambda e: e.reciprocal(out=rsc[:, 4 + r:5 + r], in_=rsc[:, r:r + 1]), reads=[rsc], writes=[rsc])
                            P.op("dve", lambda e: e.tensor_scalar(
                                out=attn[:, h, :], in0=bk[:, 0:64], scalar1=rsc[:, 4 + r:5 + r], scalar2=gates[:, 3 * h:3 * h + 1],
                                op0=ALU.mult, op1=ALU.mult), reads=[bk, rsc, gates], writes=[attn])
                            if r == 0:
                                P.op("dve", lambda e: e.tensor_scalar(out=imp[:], in0=bk[:, 65:321], scalar1=rsc[:, 4 + r:5 + r],
                                                                      scalar2=None, op0=ALU.mult), reads=[bk, rsc], writes=[imp])
                            else:
                                P.op("dve", lambda e: e.scalar_tensor_tensor(
                                    out=imp[:], in0=bk[:, 65:321], scalar=rsc[:, 4 + r:5 + r], in1=imp[:], op0=ALU.mult, op1=ALU.add),
                                    reads=[bk, rsc, imp], writes=[imp])
                            yield
                        P.op("dve", lambda e: e.tensor_tensor(out=vv[:], in0=imp[:], in1=t1t[:], op=ALU.add), reads=[imp, t1t], writes=[vv])
                        P.op("dve", lambda e: e.max(out=m8[:, 0:8], in_=vv[:]), reads=[vv], writes=[m8])
                        P.op("dve", lambda e: e.match_replace(out=vv2[:], in_to_replace=m8[:, 0:8], in_values=vv[:], imm_value=-1e30),
                             reads=[vv, m8], writes=[vv2])
                        P.op("dve", lambda e: e.max(out=m8[:, 8:16], in_=vv2[:]), reads=[vv2], writes=[m8])
                        P.op("dve", lambda e: e.tensor_scalar(out=m8[:, 16:17], in0=m8[:, 15:16], scalar1=0.0, scalar2=None, op0=ALU.max),
                             reads=[m8], writes=[m8])
                        mb2v = mb2[:].rearrange("p (a c) -> p a c", c=64)
                        P.op("pool", lambda e: e.memset(mb2v[:, :, 0:32], 0.0), writes=[mb2])
                        P.op("dve", lambda e: e.tensor_scalar(out=mb2v[:, :, 32:64], in0=vv[:].rearrange("p (a c) -> p a c", c=32),
                                                               scalar1=m8[:, 16:17], scalar2=NEG, op0=ALU.is_lt, op1=ALU.mult),
                             reads=[vv, m8], writes=[mb2])
                        yield
                        for hb_ in range(2):
                            bk = banks[6 + hb_]
                            for k4 in range(4):
                                ka = hb_ * 4 + k4
                                P.op("pe", lambda e, ka=ka, k4=k4: e.transpose(out=bk[0:64, k4 * 128:(k4 + 1) * 128],
                                                                               in_=mb2[:, ka * 64:(ka + 1) * 64], identity=ident[:]),
                                     reads=[mb2, ident], writes=[bk])
                            for r in range(4):
                                dst = AUXR[g][32:64, hb_ * 2048:(hb_ + 1) * 2048].rearrange("p (a r q) -> p a r q", r=4, q=128)[:, :, r, :]
                                src = bk[32:64, :].rearrange("p (a q) -> p a q", q=128)
                                if r % 2 == 0:
                                    P.op("act", lambda e, dst=dst, src=src: e.activation(out=dst, in_=src, func=AF.Copy), reads=[bk], writes=[AUXR[g]])
                                else:
                                    P.op("dve", lambda e, dst=dst, src=src: e.tensor_copy(out=dst, in_=src), reads=[bk], writes=[AUXR[g]])
                            yield

                def step(gen):
                    if gen is not None:
                        try:
                            next(gen)
                            return gen
                        except StopIteration:
                            return None
                    return None

                def attend2(ii, slots, kfn, vfn, biasfn, maskfn, branch, bg):
                    p = ii % 2
                    QTz_, gates, attn, AUXR = QTz2[p], gates2[p], attn2[p], AUXR2[p]
                    n = len(slots)
                    SB = [(banks[0], banks[1]), (banks[3], banks[4])]
                    OB = [banks[2], banks[5]]
                    PT = [(PTb[0], PTb[1]), (PTb[2], PTb[3])]

                    def scores(g, idx):
                        s_ = slots[idx]
                        bk = SB[g][idx % 2]
                        kb_, kap = kfn(g, idx, s_)
                        mk = maskfn(g, idx, s_)
                        P.op("pe", lambda e: e.matmul(bk[:, :], lhsT=kap, rhs=QTz_[:, g, :], start=True, stop=False),
                             reads=[kb_, QTz_], writes=[bk])
                        l_, r_ = biasfn(g, idx, s_)
                        P.op("pe", lambda e: e.matmul(bk[:, :], lhsT=l_, rhs=r_, start=False, stop=(mk is None)),
                             reads=[AUXL, AUXR[g]], writes=[bk])
                        if mk is not None:
                            for r in range(4):
                                P.op("pe", lambda e, r=r: e.matmul(bk[:, r * 128:(r + 1) * 128], lhsT=identb[:], rhs=mk[1],
                                                                   start=False, stop=(r == 3)), reads=[identb, mk[0]], writes=[bk])

                    for g in range(2):
                        scores(g, 0)
                    for idx in range(n):
                        for g in range(2):
                            bk = SB[g][idx % 2]
                            pt = PT[g][idx % 2]
                            P.op("act", lambda e, bk=bk, pt=pt: e.activation(out=pt[:], in_=bk[:, :], func=AF.Exp), reads=[bk], writes=[pt])
                        if idx + 1 < n:
                            for g in range(2):
                                scores(g, idx + 1)
                        for g in range(2):
                            pt = PT[g][idx % 2]
                            vb_, vap = vfn(g, idx, slots[idx])
                            P.op("pe", lambda e, vap=vap, pt=pt, g=g: e.matmul(OB[g][:, :], lhsT=vap, rhs=pt[:],
                                                                              start=(idx == 0), stop=(idx == n - 1)),
                                 reads=[vb_, pt], writes=[OB[g]])
                        if idx % 3 == 2:
                            bg = step(bg)
                    for g in range(2):
                        P.op("act", lambda e, g=g: e.activation(out=osb[0:65, :], in_=OB[g][0:65, :], func=AF.Copy), reads=[OB[g]], writes=[osb])
                        bk = OB[g]
                        for r in range(4):
                            P.op("pe", lambda e, r=r: e.transpose(out=bk[:, r * 65:(r + 1) * 65], in_=osb[0:65, r * 128:(r + 1) * 128],
                                                                   identity=ident[0:65, 0:65]), reads=[osb, ident], writes=[bk])
                        ov_ = bk[:, 0:260].rearrange("p (r d) -> p r d", d=65)
                        P.op("dve", lambda e: e.tensor_scalar(out=rsb[:, 8:12], in0=ov_[:, :, 64], scalar1=1e-30, scalar2=None, op0=ALU.max),
                             reads=[bk], writes=[rsb])
                        P.op("dve", lambda e: e.reciprocal(out=rsb[:, 12:16], in_=rsb[:, 8:12]), reads=[rsb], writes=[rsb])
                        gv = gates[:, 12 * g:12 * g + 12].rearrange("p (r b) -> p r b", b=3)[:, :, branch]
                        P.op("dve", lambda e: e.tensor_tensor(out=rsb[:, 8:12], in0=rsb[:, 12:16], in1=gv, op=ALU.mult),
                             reads=[rsb, gates], writes=[rsb])
                        P.op("dve", lambda e: e.tensor_tensor(out=otmp[:], in0=ov_[:, :, 0:64],
                                                               in1=rsb[:, 8:12].unsqueeze(2).broadcast_to([128, 4, 64]), op=ALU.mult),
                             reads=[bk, rsb], writes=[otmp])
                        P.op("dve", lambda e, g=g: e.tensor_tensor(out=attn[:, 4 * g:4 * g + 4, :], in0=attn[:, 4 * g:4 * g + 4, :], in1=otmp[:],
                                                                    op=ALU.add), reads=[attn, otmp], writes=[attn])
                    return bg

                Vflat = Vsel[:].rearrange("p a g d -> p (a g d)")
                gen = front(0) if len(own) > 0 else None
                while gen is not None:
                    gen = step(gen)
                for ii, i in enumerate(own):
                    p = ii % 2
                    so = 8 * i
                    AUXR, kwT_, vwb_, WMt, mixT, attn = AUXR2[p], kwT2[p], vwb2[p], WMt2[p], mixT2[p], attn2[p]
                    vwflat = vwb_[:].rearrange("p a d -> p (a d)")
                    bg = front(ii + 1) if ii + 1 < len(own) else None
                    P.dma("sp", xo[:], xr[so * 128:(so + 1) * 128, :], writes=[xo])
                    bg = attend2(ii, sel_slots(i),
                                 lambda g, idx, s_: (KselT, KselT[:, s_ * 128:(s_ + 1) * 128]),
                                 lambda g, idx, s_: (Vsel, Vflat[:, (s_ * 2 + g) * 66:(s_ * 2 + g) * 66 + 128]),
                                 lambda g, idx, s_: (AUXL[:, s_ % 16, :], AUXR[g][:, (s_ // 16) * 512:(s_ // 16 + 1) * 512]),
                                 lambda g, idx, s_: ((TRI, TRI[:, :]) if s_ == so else None), 1, bg)
                    wslots = [(so - 4 + off) % NT for off in range(5)]
                    bg = attend2(ii, wslots,
                                 lambda g, idx, s_: (kwT_, kwT_[:, idx, :]),
                                 lambda g, idx, s_: (vwb_, vwflat[:, idx * 132 + g * 66:idx * 132 + g * 66 + 128]),
                                 lambda g, idx, s_: (AUXL[0:5, s_ % 16, :], AUXR[g][0:5, (s_ // 16) * 512:(s_ // 16 + 1) * 512]),
                                 lambda g, idx, s_: (WMt, WMt[:, idx, :]), 2, bg)
                    for j in range(4):
                        P.op("pe", lambda e, j=j: e.transpose(out=banks[2][:, j * 128:(j + 1) * 128],
                                                               in_=attn[:, 2 * j:2 * j + 2, :].rearrange("p h d -> p (h d)"), identity=ident[:]),
                             reads=[attn, ident], writes=[banks[2]])
                    P.op("act", lambda e: e.activation(out=mixT[:, 4:8, :].rearrange("p a b -> p (a b)"), in_=banks[2][:, :], func=AF.Copy),
                         reads=[banks[2]], writes=[mixT])
                    for hf in range(2):
                        bk = banks[3 + hf]
                        for k in range(8):
                            P.op("pe", lambda e, k=k, bk=bk, hf=hf: e.matmul(bk[:, :], lhsT=mixT[:, k, :], rhs=Woutb[:, k, hf * 512:(hf + 1) * 512],
                                                                            start=(k == 0), stop=(k == 7)), reads=[mixT, Woutb], writes=[bk])
                        P.op("dve", lambda e, bk=bk, hf=hf: e.tensor_tensor(
                            out=xo[:, hf * 512:(hf + 1) * 512], in0=bk[:, :], in1=xo[:, hf * 512:(hf + 1) * 512], op=ALU.add),
                            reads=[bk, xo], writes=[xo])
                    P.dma("pool", out[i * 128:(i + 1) * 128, :], xo[:], reads=[xo], writes=[out_b[i]], owner=xo)
                    while bg is not None:
                        bg = step(bg)
                if stop_after <= 2:
                    P.finish("sp", [xo] + [out_b[i] for i in own])
                    return nc, dbg_out, P
            P.barrier()
        P.barrier()
        if not do_ffn:
            P.finish("sp", out_b)
            return nc, dbg_out, P
        with ExitStack() as e3:
            x1 = P.sb("x1", [128, len(own), D], F32, e3)
            x1sT = P.sb("x1sT", [128, 8, len(own) * 128], BF16, e3)
            hid = P.sb("hid", [128, 4, len(own) * 128], BF16, e3)
            W1b = [P.sb("W1b%d" % j, [128, 8, 512], BF16, e3) for j in range(2)]
            W2b = [P.sb("W2b%d" % j, [128, 4, 1024], BF16, e3) for j in range(2)]
            w1st = [P.sb("w1st%d" % j, [128, 512], F32, e3) for j in range(8)]
            w2st = [P.sb("w2st%d" % j, [128, 1024], F32, e3) for j in range(4)]
            rst = [P.sb("rst%d" % j, [128, 512], F32, e3) for j in range(2)]
            xsn2 = [P.sb("xsn%d" % j, [128, D], F32, e3) for j in range(2)]
            st32 = [P.sb("st3_%d" % j, [128, 8], F32, e3) for j in range(2)]
            b2c = [P.sb("b2c%d" % j, [128, 4], F32, e3) for j in range(2)]
            ntok = len(own) * 128
            for ii, i in enumerate(own):
                P.dma("sp", x1[:, ii, :], out[i * 128:(i + 1) * 128, :], reads=[out_b[i]], writes=[x1], nowaw=True)
            for ii, i in enumerate(own):
                xsn = xsn2[ii % 2]
                st3 = st32[ii % 2]
                P.op("act", lambda e, ii=ii: e.activation(out=xsn[:], in_=x1[:, ii, :], func=AF.Square, accum_out=st3[:, 0:1]),
                     reads=[x1], writes=[xsn, st3])
                P.op("dve", lambda e: e.tensor_scalar(out=st3[:, 1:2], in0=st3[:, 0:1], scalar1=1.0 / D, scalar2=EPS, op0=ALU.mult, op1=ALU.add),
                     reads=[st3], writes=[st3])
                P.op("act", lambda e: e.activation(out=st3[:, 2:3], in_=st3[:, 1:2], func=AF.Sqrt), reads=[st3], writes=[st3])
                P.op("dve", lambda e: e.reciprocal(out=st3[:, 3:4], in_=st3[:, 2:3]), reads=[st3], writes=[st3])
                P.op("act", lambda e, ii=ii: e.activation(out=xsn[:], in_=x1[:, ii, :], func=AF.Copy, scale=st3[:, 3:4]),
                     reads=[x1, st3], writes=[xsn])
                for k in range(8):
                    bk = banks[k // 4]
                    P.op("pe", lambda e, k=k, bk=bk: e.transpose(out=bk[:, (k % 4) * 128:(k % 4 + 1) * 128], in_=xsn[:, k * 128:(k + 1) * 128],
                                                                 identity=ident[:]), reads=[xsn, ident], writes=[bk])
                for hf in range(2):
                    src = banks[hf][:, :].rearrange("p (a q) -> p a q", q=128)
                    dst = x1sT[:, hf * 4:(hf + 1) * 4, ii * 128:(ii + 1) * 128]
                    if hf == 0:
                        P.op("act", lambda e, src=src, dst=dst: e.activation(out=dst, in_=src, func=AF.Copy), reads=[banks[hf]], writes=[x1sT])
                    else:
                        P.op("dve", lambda e, src=src, dst=dst: e.tensor_copy(out=dst, in_=src), reads=[banks[hf]], writes=[x1sT])
            ntb = (ntok + 511) // 512
            def ffn_load(fg):
                for k in range(8):
                    P.dma("sp", w1st[k][:], w_ff1[k * 128:(k + 1) * 128, fg * 512:(fg + 1) * 512], writes=[w1st[k]])
                for fc in range(4):
                    r0 = (fg * 4 + fc) * 128
                    P.dma("sp", w2st[fc][:], w_ff2[r0:r0 + 128, :], writes=[w2st[fc]])

            def ffn_conv(fg):
                W1_ = W1b[fg % 2]
                W2_ = W2b[fg % 2]
                b2_ = b2c[fg % 2]
                for k in range(8):
                    ws_ = w1st[k]
                    P.op("act", lambda e, k=k, ws_=ws_, W1_=W1_: e.activation(out=W1_[:, k, :], in_=ws_[:], func=AF.Copy, scale=cols[:, 32 + k:33 + k]),
                         reads=[ws_, cols], writes=[W1_])
                    for fc in range(4):
                        P.op("pe", lambda e, k=k, fc=fc, ws_=ws_: e.matmul(banks[4 + fc][:, 0:1], lhsT=ws_[:, fc * 128:(fc + 1) * 128],
                                                                          rhs=modcol[:, 16 + k:17 + k], start=(k == 0), stop=(k == 7)),
                             reads=[ws_, modcol], writes=[banks[4 + fc]])
                for fc in range(4):
                    P.op("dve", lambda e, fc=fc, b2_=b2_: e.tensor_copy(out=b2_[:, fc:fc + 1], in_=banks[4 + fc][:, 0:1]), reads=[banks[4 + fc]], writes=[b2_])
                for fc in range(4):
                    ws_ = w2st[fc]
                    P.op("dve", lambda e, fc=fc, ws_=ws_, W2_=W2_: e.tensor_tensor(out=W2_[:, fc, :], in0=ws_[:], in1=GA[:, 1024:2048], op=ALU.mult),
                         reads=[ws_, GA], writes=[W2_])

            ffn_load(0)
            ffn_conv(0)
            for fg in range(8):
                W1_ = W1b[fg % 2]
                W2_ = W2b[fg % 2]
                b2_ = b2c[fg % 2]
                if fg + 1 < 8:
                    ffn_load(fg + 1)
                cnt_ = 0
                for fc in range(4):
                    for tb in range(ntb):
                        t0_ = tb * 512
                        tw = min(512, ntok - t0_)
                        bk = banks[cnt_ % 2]
                        r_ = rst[cnt_ % 2]
                        for k in range(8):
                            P.op("pe", lambda e, k=k, fc=fc, bk=bk, t0_=t0_, tw=tw: e.matmul(
                                bk[:, 0:tw], lhsT=W1_[:, k, fc * 128:(fc + 1) * 128], rhs=x1sT[:, k, t0_:t0_ + tw], start=(k == 0), stop=(k == 7)),
                                reads=[W1_, x1sT], writes=[bk])
                        P.op("act", lambda e, fc=fc, bk=bk, r_=r_, tw=tw: e.activation(out=r_[:, 0:tw], in_=bk[:, 0:tw], func=AF.Relu, bias=b2_[:, fc:fc + 1]),
                             reads=[bk, b2_], writes=[r_])
                        P.op("dve", lambda e, fc=fc, r_=r_, t0_=t0_, tw=tw: e.tensor_tensor(
                            out=hid[:, fc, t0_:t0_ + tw], in0=r_[:, 0:tw], in1=r_[:, 0:tw], op=ALU.mult), reads=[r_], writes=[hid])
                        cnt_ += 1
                if fg + 1 < 8:
                    ffn_conv(fg + 1)
                for ii in range(len(own)):
                    for hf in range(2):
                        bk = banks[(ii * 2 + hf) % 4]
                        for fc in range(4):
                            P.op("pe", lambda e, fc=fc, bk=bk, ii=ii, hf=hf: e.matmul(
                                bk[:, :], lhsT=hid[:, fc, ii * 128:(ii + 1) * 128], rhs=W2_[:, fc, hf * 512:(hf + 1) * 512], start=(fc == 0), stop=(fc == 3)),
                                reads=[hid, W2_], writes=[bk])
                        P.op("dve", lambda e, bk=bk, ii=ii, hf=hf: e.tensor_tensor(
                            out=x1[:, ii, hf * 512:(hf + 1) * 512], in0=bk[:, :], in1=x1[:, ii, hf * 512:(hf + 1) * 512], op=ALU.add),
                            reads=[bk, x1], writes=[x1])
            for ii, i in enumerate(own):
                P.dma("sp", out[i * 128:(i + 1) * 128, :], x1[:, ii, :], reads=[x1], writes=[out_b[i]], owner=x1)
            P.finish("sp", [x1] + out_b)
            P.barrier()
    return nc, dbg_out, P


def make_inputs(inp, c):
    f = lambda a: np.ascontiguousarray(np.asarray(a, np.float32))
    x = np.asarray(inp["x"], np.float32)[0]
    m = {}
    m["xr"] = f(np.roll(x.reshape(NT, 128, D), -c, axis=0).reshape(S, D))
    m["ccol"] = f(np.asarray(inp["c"])[0].reshape(8, 128).T)
    m["w_ada"] = f(inp["w_ada"][0])
    m["b_ada"] = f(np.asarray(inp["b_ada"])[0][None, :])
    m["g1col"] = f(np.asarray(inp["norm1_g"])[0].reshape(8, 128).T)
    m["g2col"] = f(np.asarray(inp["norm2_g"])[0].reshape(8, 128).T)
    m["w_in"] = f(inp["w_in"][0])
    m["w_pool"] = f(np.asarray(inp["w_pool"])[0].transpose(1, 0, 2))
    sc = np.zeros((128, 16), np.float32)
    for j, nm in enumerate(("ks_gain", "kw_gain", "kc_gain", "q_gain")):
        sc[:, j] = np.tile(np.asarray(inp[nm])[0], 2)
    sc[:, 4:8] = np.asarray(inp["pool_scale"])[0].reshape(4, 128).T
    sc[:, 8:10] = np.asarray(inp["cmp_b1_k"])[0].reshape(2, 128).T
    sc[:, 10:12] = np.asarray(inp["cmp_b1_v"])[0].reshape(2, 128).T
    m["smallc"] = sc
    m["cmp_w1_k"] = f(np.asarray(inp["cmp_w1_k"])[0].reshape(32, 64, 256).transpose(1, 0, 2))
    m["cmp_w1_v"] = f(np.asarray(inp["cmp_w1_v"])[0].reshape(32, 64, 256).transpose(1, 0, 2))
    w2 = np.stack([np.asarray(inp["cmp_w2_k"])[0], np.asarray(inp["cmp_w2_v"])[0]], 0)
    m["cmp_w2"] = f(w2.reshape(2, 2, 128, 64).transpose(2, 0, 1, 3))
    m["cmp_b2"] = f(np.concatenate([np.asarray(inp["cmp_b2_k"])[0], np.asarray(inp["cmp_b2_v"])[0]])[None, :])
    m["cmp_b1row"] = f(np.concatenate([np.asarray(inp["cmp_b1_k"])[0], np.asarray(inp["cmp_b1_v"])[0]])[None, :])
    pos = np.stack([np.asarray(inp["cmp_pos_k"])[0], np.asarray(inp["cmp_pos_v"])[0]], 0)
    m["cmp_posT"] = f(pos.transpose(2, 0, 1))
    m["w_out"] = f(inp["w_out"][0])
    m["w_ff1"] = f(inp["w_ff1"][0])
    m["w_ff2"] = f(inp["w_ff2"][0])
    bf = ("ov", "auxl", "auxr", "alcr", "cmask", "alcl", "wm")
    for k, v in shared_tables().items():
        m[k] = v.astype(ml_dtypes.bfloat16) if k in bf else v
    for k, v in core_tables(c).items():
        m[k] = v.astype(ml_dtypes.bfloat16) if k in bf else v
    return m


_CACHE = {}


def kernel(**inputs):
    if "nc" not in _CACHE:
        _CACHE["nc"] = build({})[0]
    nc = _CACHE["nc"]
    in_maps = [make_inputs(inputs, c) for c in range(NCORES)]
    res = run_bass_kernel_spmd(nc, in_maps, core_ids=list(range(NCORES)))
    full = np.zeros((NT, 128, D), np.float32)
    for c in range(NCORES):
        o = np.asarray(res.results[c]["out"]).reshape(NOWN, 128, D)
        for i in range(NOWN):
            full[c + 8 * i] = o[i]
    return full.reshape(1, S, D)
```

```python
import numpy as np
import ml_dtypes
from contextlib import ExitStack
import concourse.bass as bass
import concourse.mybir as mybir
from concourse.bass_utils import run_bass_kernel_spmd

F32 = mybir.dt.float32
BF16 = mybir.dt.bfloat16
AF = mybir.ActivationFunctionType
ALU = mybir.AluOpType
AX = mybir.AxisListType

S = 16384
D = 1024
NT = 128
NCORES = 8
NOWN = 16
EPS = 1e-6
NEG = -50000.0
DFF = 4096


class Buf:
    __slots__ = ("t", "w", "r", "dsem", "dcnt", "name", "excl")

    def __init__(self, t, name="", excl=False):
        self.excl = excl
        self.t = t
        self.w = {}
        self.r = {}
        self.dsem = None
        self.dcnt = 0
        self.name = name

    def __getitem__(self, k):
        return self.t[k]


class _Stop(Exception):
    pass


class Prog:
    def __init__(self, nc, es):
        self.nc = nc
        self.es = es
        self.eng = {"pe": nc.tensor, "act": nc.scalar, "dve": nc.vector, "pool": nc.gpsimd, "sp": nc.sync}
        self.sem = {}
        self.cnt = {}
        for k in self.eng:
            self.sem[k] = es.enter_context(nc.semaphore("s_" + k))
            self.cnt[k] = 0
        self.waited = {k: {} for k in self.eng}
        self.nd = 0
        self.nwaits = 0
        self.nins = 0
        self.dlast = {}

    def sb(self, name, shape, dt, es=None):
        self.nt = getattr(self, "nt", 0) + 1
        if not hasattr(self, "names"):
            self.names = {}
        self.names[name] = "t%d_%s" % (self.nt, name)
        t = (es or self.es).enter_context(self.nc.sbuf_tensor("t%d_%s" % (self.nt, name), list(shape), dt))
        return Buf(t, name)

    def ps(self, name, shape, dt=F32):
        t = self.es.enter_context(self.nc.psum_tensor(name, list(shape), dt))
        return Buf(t, name, excl=True)

    def _deps(self, reads, writes):
        d = {}
        for b in reads:
            for k, v in b.w.items():
                if d.get(k, 0) < v:
                    d[k] = v
        for b in writes:
            for k, v in b.w.items():
                if d.get(k, 0) < v:
                    d[k] = v
            for k, v in b.r.items():
                if d.get(k, 0) < v:
                    d[k] = v
        return d

    def _emit_waits(self, e, deps):
        eng = self.eng[e]
        wd = self.waited[e]
        for k, v in deps.items():
            if k == e and e == "pe":
                continue
            if wd.get(k, 0) >= v:
                continue
            eng.wait_ge(self.sem[k], v)
            self.nwaits += 1
            wd[k] = v

    stopped = False

    def op(self, e, fn, reads=(), writes=()):
        if self.stopped:
            return 0
        if any(b.excl for b in reads):
            writes = list(writes) + [b for b in reads if b.excl and b not in writes]
            reads = [b for b in reads if not b.excl]
        self._emit_waits(e, self._deps(reads, writes))
        ins = fn(self.eng[e])
        ins.then_inc(self.sem[e], 1)
        self.cnt[e] += 1
        self.nins += 1
        tok = self.cnt[e]
        for b in writes:
            b.w = {e: tok}
            b.r = {}
        for b in reads:
            if b.r.get(e, 0) < tok:
                b.r[e] = tok
        return tok

    def dma(self, q, out_ap, in_ap, reads=(), writes=(), owner=None, nowaw=False):
        if self.stopped:
            return
        tgt = owner if owner is not None else (writes[0] if writes else reads[0])
        deps = self._deps(reads, writes)
        if nowaw and tgt.dsem is not None:
            deps.pop(tgt.dsem, None)
        self._emit_waits(q, deps)
        if tgt.dsem is None:
            self.nd += 1
            key = "d%d" % self.nd
            self.sem[key] = self.es.enter_context(self.nc.semaphore(key))
            tgt.dsem = key
        key = tgt.dsem
        ins = self.eng[q].dma_start(out=out_ap, in_=in_ap)
        ins.then_inc(self.sem[key], 16)
        tgt.dcnt += 16
        self.nins += 1
        tok = tgt.dcnt
        self.dlast[key] = tok
        for b in writes:
            b.w = {key: tok}
            b.r = {}
        for b in reads:
            if not nowaw and b.r.get(key, 0) < tok:
                b.r[key] = tok

    def barrier(self):
        if self.stopped:
            return
        deps = {k: v for k, v in self.cnt.items() if v > 0}
        deps.update(self.dlast)
        for e in self.eng:
            self._emit_waits(e, dict(deps))

    def finish(self, e, bufs):
        if self.stopped:
            return
        deps = {}
        for b in bufs:
            for dd in (b.w, b.r):
                for k, v in dd.items():
                    if deps.get(k, 0) < v:
                        deps[k] = v
        self._emit_waits(e, deps)


def _slopes():
    return np.array([2.0 ** (-(h + 1)) for h in range(8)], np.float64)


def cmp_chunks(i):
    nci = (i + 2) // 2
    return sorted(set(list(range(nci)) + [7]))


def sel_slots(i):
    return list(range(0, 8 * i + 1)) + list(range(121, 128))


def core_tables(c):
    sl = _slopes()
    ql = np.arange(128)
    T = {}
    t1 = np.zeros((NOWN, 128, 256), np.float32)
    jp = np.arange(256)
    jg = (jp + 2 * c) % 256
    for i in range(NOWN):
        t = 128 * (8 * i + c) + ql
        cur = t // 64
        causal = jg[None, :] <= cur[:, None]
        forced = (jg[None, :] == 0) | (jg[None, :] == cur[:, None]) | (jg[None, :] == cur[:, None] - 1)
        t1[i] = np.where(causal, 1e4 * forced, -1.0)
    T["t1"] = t1
    cm = np.zeros((NOWN, 128, 8, 128), np.float32)
    npr = np.arange(1024)
    ng = (npr + 8 * c) % 1024
    for i in range(NOWN):
        t = 128 * (8 * i + c) + ql
        valid = (ng[:, None] <= 1022) & ((16 * ng[:, None] + 31) <= t[None, :])
        m = np.where(valid, 0.0, NEG).astype(np.float32)
        cm[i] = m.reshape(8, 128, 128).transpose(1, 0, 2)
    T["cmask"] = cm
    nl = np.arange(128)
    alc = np.zeros((4, 2, 128), np.float32)
    alc[0, :, :] = 16 * nl
    alc[1, :, :] = 1
    alc[2, :, :] = 1
    alc[3, 1, :] = ((896 + nl) >= (1024 - 8 * c)).astype(np.float32)
    T["alcl"] = alc
    wm = np.zeros((2, 5, 128, 128), np.float32)
    kl = np.arange(128)
    for st in range(2):
        for off in range(5):
            d = 128 * (4 - off) + ql[None, :] - kl[:, None]
            ok = (d >= 0) & (d < 512)
            if st == 0:
                gt = c - 4 + off
                if gt < 0:
                    ok = np.zeros_like(ok)
            wm[st, off] = np.where(ok, 0.0, NEG)
    T["wm"] = wm.transpose(2, 0, 1, 3).copy()
    pm = np.zeros((2, 4, 128, 128), np.float32)
    ph = np.zeros((2, 4, 32, 128), np.float32)
    for st in range(2):
        first = (st == 0 and c == 0)
        for g, w in enumerate((2, 4, 8, 16)):
            for tt in range(128):
                cnt = min(tt + 1, w) if first else w
                for j in range(tt - w + 1, tt + 1):
                    if j >= 0:
                        pm[st, g, j, tt] += 1.0 / cnt
                    elif not first:
                        ph[st, g, 32 + j, tt] += 1.0 / cnt
                pm[st, g, tt, tt] -= 1.0
    T["pm"] = pm.transpose(2, 0, 1, 3).copy()
    T["ph"] = ph.transpose(2, 0, 1, 3).copy()
    return T


_SHARED = None


def shared_tables():
    global _SHARED
    if _SHARED is not None:
        return _SHARED
    sl = _slopes()
    ql = np.arange(128)
    T = {}
    T["ident"] = np.eye(128, dtype=np.float32)
    ov = np.zeros((1024, 256), np.float32)
    for n in range(1024):
        for j in range(256):
            for wrap in (0, 256):
                o = min(16 * n + 32, 64 * (j + wrap) + 64) - max(16 * n, 64 * (j + wrap))
                if o > 0:
                    ov[n, j] += o / 32.0
    T["ov"] = ov.reshape(8, 128, 256).transpose(1, 0, 2).copy()
    auxl = np.zeros((64, 16, 128), np.float32)
    kl = np.arange(128)
    for kb in range(16):
        for j in range(32):
            auxl[32 + j, kb, :] = (j == 2 * kb + kl // 64)
        auxl[0, kb, :] = 128 * kb
        auxl[1, kb, :] = kl
        auxl[2, kb, :] = 1
        auxl[3, kb, :] = 1
        auxl[4, kb, :] = 1.0 if kb >= 9 else 0.0
    T["auxl"] = auxl
    auxr = np.zeros((NOWN, 2, 5, 8, 4, 128), np.float32)
    alcr = np.zeros((NOWN, 2, 4, 8, 4, 128), np.float32)
    for i in range(NOWN):
        for g in range(2):
            for r in range(4):
                s_ = sl[4 * g + r]
                for ka in range(8):
                    auxr[i, g, 0, ka, r, :] = s_
                    auxr[i, g, 1, ka, r, :] = s_
                    auxr[i, g, 2, ka, r, :] = -s_ * 128 * (8 * i - 16 * ka)
                    auxr[i, g, 3, ka, r, :] = -s_ * ql
                    auxr[i, g, 4, ka, r, :] = -s_ * 16384 if ka == 7 else 0.0
                    alcr[i, g, 0, ka, r, :] = s_
                    alcr[i, g, 1, ka, r, :] = -s_ * (ql - 31)
                    alcr[i, g, 2, ka, r, :] = -s_ * 128 * (8 * i - 16 * ka)
                    alcr[i, g, 3, ka, r, :] = -s_ * 16384 if ka == 7 else 0.0
    T["auxr"] = auxr.reshape(NOWN, 2, 5, 8 * 512)
    T["alcr"] = alcr.reshape(NOWN, 2, 4, 8 * 512)
    _SHARED = T
    return T


def build(cfg):
    own = cfg.get("own", list(range(NOWN)))
    p1 = cfg.get("p1", list(range(NT)))
    do_ffn = cfg.get("ffn", True)
    dbg = cfg.get("dbg", False)
    stop_after = cfg.get("stop_after", 99)

    nc = bass.Bass("TRN2", target_bir_lowering=False)

    def din(name, shape, dt=F32):
        return nc.dram_tensor(name, list(shape), dt, kind="ExternalInput").ap()

    xr = din("xr", [S, D])
    ccol_d = din("ccol", [128, 8])
    wada = din("w_ada", [D, 6 * D])
    bada = din("b_ada", [1, 6 * D])
    g1c_d = din("g1col", [128, 8])
    g2c_d = din("g2col", [128, 8])
    w_in = din("w_in", [D, 1816])
    wpool_d = din("w_pool", [128, 4, 128])
    smallc_d = din("smallc", [128, 16])
    w1k_d = din("cmp_w1_k", [64, 32, 256])
    w1v_d = din("cmp_w1_v", [64, 32, 256])
    w2_d = din("cmp_w2", [128, 2, 2, 64])
    b2_d = din("cmp_b2", [1, 128])
    b1r_d = din("cmp_b1row", [1, 512])
    pos_d = din("cmp_posT", [64, 2, 32])
    w_out = din("w_out", [D, D])
    w_ff1 = din("w_ff1", [D, DFF])
    w_ff2 = din("w_ff2", [DFF, D])
    ident_d = din("ident", [128, 128])
    ov_d = din("ov", [128, 8, 256], BF16)
    auxl_d = din("auxl", [64, 16, 128], BF16)
    auxr_d = din("auxr", [NOWN, 2, 5, 4096], BF16)
    alcr_d = din("alcr", [NOWN, 2, 4, 4096], BF16)
    t1_d = din("t1", [NOWN, 128, 256])
    cmask_d = din("cmask", [NOWN, 128, 8, 128], BF16)
    alcl_d = din("alcl", [4, 2, 128], BF16)
    wm_d = din("wm", [128, 2, 5, 128], BF16)
    pm_d = din("pm", [128, 2, 4, 128])
    ph_d = din("ph", [32, 2, 4, 128])
    out = nc.dram_tensor("out", [NOWN * 128, D], F32, kind="ExternalOutput").ap()
    kw_scr = nc.dram_tensor("kw_scr", [NT, 128, 128], BF16, kind="Internal").ap()
    vw_scr = nc.dram_tensor("vw_scr", [NT, 128, 132], BF16, kind="Internal").ap()
    dbg_out = {}

    def dout(name, shape):
        dbg_out[name] = nc.dram_tensor(name, list(shape), F32, kind="ExternalOutput").ap()
        return dbg_out[name]

    with ExitStack() as es:
      P = Prog(nc, es)
      try:
        return _build_body(nc, P, es, cfg, locals())
      except _Stop:
        return nc, {}, P


def _build_body(nc, P, es, cfg, L):
    globals().update({k: v for k, v in L.items() if k not in ("nc", "P", "es", "cfg")})
    own = L["own"]; p1 = L["p1"]; do_ffn = L["do_ffn"]; dbg = L["dbg"]; stop_after = L["stop_after"]; dbg_out = L["dbg_out"]; dout = L["dout"]
    if True:
        banks = [P.ps("bk%d" % i, [128, 512], F32) for i in range(8)]
        ident = P.sb("ident", [128, 128], F32)
        identb = P.sb("identb", [128, 128], BF16)
        ones = P.sb("ones", [128, 128], F32)
        cols = P.sb("cols", [128, 64], F32)
        modcol = P.sb("modcol", [128, 32], F32)
        GA = P.sb("GA", [128, 2048], F32)
        kw_scr_b = [Buf(kw_scr, "kwscr%d" % s) for s in range(NT)]
        vw_scr_b = [Buf(vw_scr, "vwscr%d" % s) for s in range(NT)]
        out_b = [Buf(out, "out%d" % i) for i in range(NOWN)]

        P.dma("sp", ident[:], ident_d[:, :], writes=[ident])
        P.op("dve", lambda e: e.tensor_copy(out=identb[:], in_=ident[:]), reads=[ident], writes=[identb])
        P.op("pool", lambda e: e.memset(ones[:], 1.0), writes=[ones])
        P.dma("sp", cols[:, 0:8], ccol_d[:, :], writes=[cols])
        P.dma("sp", cols[:, 8:16], g1c_d[:, :], writes=[cols])
        P.dma("sp", cols[:, 16:24], g2c_d[:, :], writes=[cols])
        P.dma("sp", cols[:, 40:56], smallc_d[:, :], writes=[cols])
        P.op("dve", lambda e: e.tensor_scalar(out=cols[:, 56:57], in0=cols[:, 43:44], scalar1=0.125, scalar2=None, op0=ALU.mult),
             reads=[cols], writes=[cols])

        with ExitStack() as e0:
            modrow = P.sb("modrow", [1, 6 * D], F32, e0)
            badar = P.sb("badar", [1, 6 * D], F32, e0)
            wst = [P.sb("wst%d" % j, [128, 8, 512], F32, e0) for j in range(4)]
            P.dma("sp", badar[:], bada[:, :], writes=[badar])
            wada_v = wada.rearrange("(k p) n -> p k n", p=128)
            for n in range(12):
                ws_ = wst[n % 4]
                P.dma("sp", ws_[:], wada_v[:, :, n * 512:(n + 1) * 512], writes=[ws_])
                bk = banks[n % 2]
                for k in range(8):
                    P.op("pe", lambda e, k=k, bk=bk, ws_=ws_: e.matmul(bk[0:1, 0:512], lhsT=cols[:, k:k + 1], rhs=ws_[:, k, :],
                                                                     start=(k == 0), stop=(k == 7)),
                         reads=[cols, ws_], writes=[bk])
                P.op("dve", lambda e, n=n, bk=bk: e.tensor_tensor(out=modrow[0:1, n * 512:(n + 1) * 512], in0=bk[0:1, 0:512],
                                                                  in1=badar[0:1, n * 512:(n + 1) * 512], op=ALU.add),
                     reads=[bk, badar], writes=[modrow])
            bk = banks[2]
            for j in range(32):
                off = [0, 1024, 3072, 4096][j // 8] + (j % 8) * 128
                P.op("pe", lambda e, j=j, off=off: e.matmul(bk[:, j:j + 1], lhsT=modrow[0:1, off:off + 128], rhs=ones[0:1, 0:1],
                                                            start=True, stop=True),
                     reads=[modrow, ones], writes=[bk])
            P.op("dve", lambda e: e.tensor_copy(out=modcol[:], in_=bk[:, 0:32]), reads=[bk], writes=[modcol])
            for h in range(4):
                off = (2048 if h < 2 else 5120) + (h % 2) * 512
                bk2 = banks[3 + h % 2]
                P.op("pe", lambda e, off=off, bk2=bk2: e.matmul(bk2[:, 0:512], lhsT=ones[0:1, 0:128], rhs=modrow[0:1, off:off + 512],
                                                                start=True, stop=True),
                     reads=[modrow, ones], writes=[bk2])
                P.op("act", lambda e, h=h, bk2=bk2: e.activation(out=GA[:, h * 512:(h + 1) * 512], in_=bk2[:, 0:512], func=AF.Copy),
                     reads=[bk2], writes=[GA])
            for (dst, gsrc, scsrc) in ((24, 8, 8), (32, 16, 24)):
                P.op("dve", lambda e, dst=dst, gsrc=gsrc, scsrc=scsrc: e.scalar_tensor_tensor(
                    out=cols[:, dst:dst + 8], in0=modcol[:, scsrc:scsrc + 8], scalar=1.0, in1=cols[:, gsrc:gsrc + 8],
                    op0=ALU.add, op1=ALU.mult), reads=[modcol, cols], writes=[cols])

        P.barrier()

        def prep_win(lo, hi, Wdst, BIASdst, esx, doff=0):
            ncol = hi - lo
            chunks = []
            o = 0
            while o < ncol:
                w_ = min(512, ncol - o)
                chunks.append((o, w_))
                o += w_
            with ExitStack() as el:
                wst2 = [P.sb("wst2_%d" % j, [128, ncol], F32, el) for j in range(3)]
                brow = P.sb("brow", [1, ncol], F32, el)
                for k in range(8):
                    ws_ = wst2[k % 3]
                    P.dma("sp", ws_[:], w_in[k * 128:(k + 1) * 128, lo:hi], writes=[ws_])
                    P.op("dve", lambda e, k=k, ws_=ws_: e.tensor_scalar(out=Wdst[:, k, doff:doff + ncol], in0=ws_[:], scalar1=cols[:, 24 + k:25 + k],
                                                                         scalar2=None, op0=ALU.mult),
                         reads=[ws_, cols], writes=[Wdst])
                    for ci, (o, w_) in enumerate(chunks):
                        bk = banks[4 + ci]
                        P.op("pe", lambda e, k=k, bk=bk, o=o, w_=w_, ws_=ws_: e.matmul(bk[0:1, 0:w_], lhsT=modcol[:, k:k + 1],
                                                                                      rhs=ws_[:, o:o + w_], start=(k == 0), stop=(k == 7)),
                             reads=[modcol, ws_], writes=[bk])
                for ci, (o, w_) in enumerate(chunks):
                    bk = banks[4 + ci]
                    P.op("dve", lambda e, bk=bk, o=o, w_=w_: e.tensor_copy(out=brow[0:1, o:o + w_], in_=bk[0:1, 0:w_]),
                         reads=[bk], writes=[brow])
                for ci, (o, w_) in enumerate(chunks):
                    bk = banks[4 + ci]
                    P.op("pe", lambda e, bk=bk, o=o, w_=w_: e.matmul(bk[:, 0:w_], lhsT=ones[0:1, 0:128], rhs=brow[0:1, o:o + w_],
                                                                    start=True, stop=True), reads=[ones, brow], writes=[bk])
                    P.op("act", lambda e, bk=bk, o=o, w_=w_: e.activation(out=BIASdst[:, doff + o:doff + o + w_], in_=bk[:, 0:w_], func=AF.Copy),
                         reads=[bk], writes=[BIASdst])
            P.barrier()

        with ExitStack() as eatt:
            KselT = P.sb("KselT", [128, S], BF16, eatt)
            Vsel = P.sb("Vsel", [128, NT + 1, 2, 66], BF16, eatt)
            KcmpT = P.sb("KcmpT", [128, 1024], BF16, eatt)
            VOc = P.sb("VOc", [128, 8, 2, 322], BF16, eatt)
            P.op("pool", lambda e: e.memset(Vsel[:, NT, :, :], 0.0), writes=[Vsel])
            P.op("pool", lambda e: e.memset(Vsel[:, :, :, 64:66], 1.0), writes=[Vsel])
            P.op("pool", lambda e: e.memset(VOc[:, :, :, 64:65], 1.0), writes=[VOc])
            for g in range(2):
                P.dma("sp", VOc[:, :, g, 65:321], ov_d[:, :, :], writes=[VOc])

            with ExitStack() as e1:
                Wkv = P.sb("Wkv", [128, 8, 768], BF16, e1)
                BIASkv = P.sb("BIASkv", [128, 768], F32, e1)
                prep_win(1024, 1792, Wkv, BIASkv, e1)
                W1c = [P.sb("W1c%d" % kv, [128, 32, 256], BF16, e1) for kv in range(2)]
                W2c = P.sb("W2c", [128, 2, 2, 64], BF16, e1)
                B2c = P.sb("B2c", [128, 128], F32, e1)
                posT = P.sb("posT", [128, 2, 32], BF16, e1)
                B1bc = P.sb("B1bc", [128, 512], F32, e1)
                cT = [[P.sb("cT%d_%d" % (kv, b), [128, 2176], BF16, e1) for b in range(2)] for kv in range(2)]
                for kv in range(2):
                    for b_ in range(2):
                        P.op("pool", lambda e, kv=kv, b_=b_: e.memset(cT[kv][b_][:], 0.0), writes=[cT[kv][b_]])
                for kv, wd in enumerate((w1k_d, w1v_d)):
                    with ExitStack() as el2:
                        st2 = P.sb("w1st%d" % kv, [128, 32, 256], F32, el2)
                        P.dma("sp", st2[0:64], wd[:, :, :], writes=[st2])
                        P.dma("sp", st2[64:128], wd[:, :, :], writes=[st2])
                        P.op("dve", lambda e, kv=kv, st2=st2: e.tensor_copy(out=W1c[kv][:], in_=st2[:]), reads=[st2], writes=[W1c[kv]])
                    P.barrier()
                with ExitStack() as el:
                    w2st = P.sb("w2st", [128, 2, 2, 64], F32, el)
                    P.dma("sp", w2st[:], w2_d[:, :, :, :], writes=[w2st])
                    P.op("dve", lambda e: e.tensor_copy(out=W2c[:], in_=w2st[:]), reads=[w2st], writes=[W2c])
                    b2row = P.sb("b2row", [1, 128], F32, el)
                    P.dma("sp", b2row[:], b2_d[:, :], writes=[b2row])
                    P.op("pe", lambda e: e.matmul(banks[0][:, 0:128], lhsT=ones[0:1, 0:128], rhs=b2row[0:1, :], start=True, stop=True),
                         reads=[ones, b2row], writes=[banks[0]])
                    P.op("act", lambda e: e.activation(out=B2c[:], in_=banks[0][:, 0:128], func=AF.Copy), reads=[banks[0]], writes=[B2c])
                    posst = P.sb("posst", [128, 2, 32], F32, el)
                    P.dma("sp", posst[0:64], pos_d[:, :, :], writes=[posst])
                    P.dma("sp", posst[64:128], pos_d[:, :, :], writes=[posst])
                    P.op("dve", lambda e: e.tensor_copy(out=posT[:], in_=posst[:]), reads=[posst], writes=[posT])
                    b1row = P.sb("b1row", [1, 512], F32, el)
                    b1eff = P.sb("b1eff", [1, 512], F32, el)
                    P.dma("sp", b1row[:], b1r_d[:, :], writes=[b1row])
                    bk = banks[1]
                    for kv in range(2):
                        for l in range(32):
                            P.op("pe", lambda e, kv=kv, l=l: e.matmul(bk[0:1, kv * 256:(kv + 1) * 256], lhsT=posT[0:64, kv, l:l + 1],
                                                                      rhs=W1c[kv][0:64, l, :], start=(l == 0), stop=(l == 31)),
                                 reads=[W1c[kv], posT], writes=[bk])
                    P.op("dve", lambda e: e.tensor_tensor(out=b1eff[:], in0=bk[0:1, :], in1=b1row[:], op=ALU.add), reads=[bk, b1row], writes=[b1eff])
                    P.op("pe", lambda e: e.matmul(banks[0][:, :], lhsT=ones[0:1, 0:128], rhs=b1eff[0:1, :], start=True, stop=True),
                         reads=[ones, b1eff], writes=[banks[0]])
                    P.op("act", lambda e: e.activation(out=B1bc[:], in_=banks[0][:, :], func=AF.Copy), reads=[banks[0]], writes=[B1bc])
                P.barrier()

                xs = [P.sb("xs%d" % j, [128, D], F32, e1) for j in range(4)]
                xT = [P.sb("xT%d" % j, [128, 8, 128], BF16, e1) for j in range(3)]
                junk = P.sb("junk", [128, D], F32, e1)
                st = [P.sb("st%d" % j, [128, 16], F32, e1) for j in range(3)]
                kvb = [P.sb("kvb%d" % j, [128, 768], F32, e1) for j in range(2)]
                sq = P.sb("sq", [128, 2, 2, 64], F32, e1)
                kn = [P.sb("kn%d" % j, [128, 2, 2, 64], F32, e1) for j in range(2)]
                kwst = [P.sb("kwst%d" % j, [128, 128], BF16, e1) for j in range(2)]
                vwst = [P.sb("vwst%d" % j, [128, 2, 66], BF16, e1) for j in range(2)]
                for j in range(2):
                    P.op("pool", lambda e, j=j: e.memset(vwst[j][:, :, 64:66], 1.0), writes=[vwst[j]])
                hidT = P.sb("hidT", [128, 8, 128], BF16, e1)
                spill0 = P.sb("spill0", [128, 2, 128], BF16, e1)
                hpre = P.sb("hpre", [128, 2, 512], F32, e1)
                hgel = P.sb("hgel", [128, 2, 512], F32, e1)
                ctok = P.sb("ctok", [128, 2, 2, 64], F32, e1)
                ckn = P.sb("ckn", [128, 2, 64], F32, e1)
                cst = P.sb("cst", [128, 8], F32, e1)
                csq = P.sb("csq", [128, 2, 64], F32, e1)

                def compress(j):
                    b = j % 2
                    hb = [banks[6], banks[7]]
                    for g in range(2):
                        bk = hb[g]
                        for kv in range(2):
                            for l in range(32):
                                P.op("pe", lambda e, kv=kv, g=g, l=l, bk=bk: e.matmul(
                                    bk[:, kv * 256:(kv + 1) * 256], lhsT=cT[kv][b][g * 64:(g + 1) * 64, l:l + 16 * 127 + 1:16],
                                    rhs=W1c[kv][g * 64:(g + 1) * 64, l, :], start=(l == 0), stop=(l == 31)),
                                    reads=[W1c[kv], cT[kv][b]], writes=[bk])
                    for g in range(2):
                        bk = hb[g]
                        P.op("dve", lambda e, g=g, bk=bk: e.tensor_tensor(out=hpre[:, g, :], in0=bk[:, :], in1=B1bc[:], op=ALU.add),
                             reads=[bk, B1bc], writes=[hpre])
                        P.op("act", lambda e, g=g: e.activation(out=hgel[:, g, :], in_=hpre[:, g, :], func=AF.Gelu_apprx_tanh),
                             reads=[hpre], writes=[hgel])
                    hid5 = hidT[:].rearrange("p (kv g ch) n -> p kv g ch n", kv=2, g=2)
                    for g in range(2):
                        bk = hb[g]
                        for t4 in range(4):
                            P.op("pe", lambda e, g=g, t4=t4, bk=bk: e.transpose(out=bk[:, t4 * 128:(t4 + 1) * 128], in_=hgel[:, g, t4 * 128:(t4 + 1) * 128],
                                                                                identity=ident[:]), reads=[hgel, ident], writes=[bk])
                        src = bk[:, :].rearrange("p (kv ch n) -> p kv ch n", kv=2, ch=2)
                        if g == 0:
                            P.op("act", lambda e, src=src: e.activation(out=hid5[:, :, 0, :, :], in_=src, func=AF.Copy), reads=[bk], writes=[hidT])
                        else:
                            P.op("dve", lambda e, src=src: e.tensor_copy(out=hid5[:, :, 1, :, :], in_=src), reads=[bk], writes=[hidT])
                    bk = banks[4]
                    for kv in range(2):
                        for g in range(2):
                            o = (kv * 2 + g) * 64
                            for ch in range(2):
                                idx = (kv * 2 + g) * 2 + ch
                                P.op("pe", lambda e, kv=kv, ch=ch, idx=idx, o=o: e.matmul(
                                    bk[:, o:o + 64], lhsT=hidT[:, idx, :], rhs=W2c[:, kv, ch, :], start=(ch == 0), stop=(ch == 1)),
                                    reads=[hidT, W2c], writes=[bk])
                    for kv in range(2):
                        P.op("dve", lambda e, kv=kv: e.tensor_tensor(
                            out=ctok[:, kv, :, :], in0=bk[:, kv * 128:(kv + 1) * 128].rearrange("p (g d) -> p g d", g=2),
                            in1=B2c[:, kv * 64:(kv + 1) * 64].unsqueeze(1).broadcast_to([128, 2, 64]), op=ALU.add),
                            reads=[bk, B2c], writes=[ctok])
                    P.op("dve", lambda e: e.tensor_tensor(out=csq[:], in0=ctok[:, 0, :, :], in1=ctok[:, 0, :, :], op=ALU.mult),
                         reads=[ctok], writes=[csq])
                    P.op("dve", lambda e: e.tensor_reduce(out=cst[:, 0:2], in_=csq[:], axis=AX.X, op=ALU.add), reads=[csq], writes=[cst])
                    P.op("dve", lambda e: e.tensor_scalar(out=cst[:, 2:4], in0=cst[:, 0:2], scalar1=1.0 / 64, scalar2=EPS,
                                                           op0=ALU.mult, op1=ALU.add), reads=[cst], writes=[cst])
                    P.op("act", lambda e: e.activation(out=cst[:, 4:6], in_=cst[:, 2:4], func=AF.Sqrt), reads=[cst], writes=[cst])
                    P.op("dve", lambda e: e.reciprocal(out=cst[:, 6:8], in_=cst[:, 4:6]), reads=[cst], writes=[cst])
                    P.op("dve", lambda e: e.tensor_tensor(out=ckn[:], in0=ctok[:, 0, :, :],
                                                           in1=cst[:, 6:8].unsqueeze(2).broadcast_to([128, 2, 64]), op=ALU.mult),
                         reads=[ctok, cst], writes=[ckn])
                    P.op("pe", lambda e: e.transpose(out=bk[:, 256:384], in_=ckn[:].rearrange("p g d -> p (g d)"), identity=ident[:]),
                         reads=[ckn, ident], writes=[bk])
                    P.op("act", lambda e: e.activation(out=KcmpT[:, j * 128:(j + 1) * 128], in_=bk[:, 256:384], func=AF.Copy,
                                                        scale=cols[:, 42:43]), reads=[bk, cols], writes=[KcmpT])
                    P.op("pool", lambda e: e.tensor_copy(out=VOc[:, j, :, 0:64], in_=ctok[:, 1, :, :]), reads=[ctok], writes=[VOc])

                nsl = len(p1)

                def loadx(si):
                    s = p1[si]
                    P.dma("sp", xs[si % 4][:], xr[s * 128:(s + 1) * 128, :], writes=[xs[si % 4]])

                def stageA1(si):
                    s = p1[si]
                    x_ = xs[si % 4]
                    xT_ = xT[si % 3]
                    st_ = st[si % 3]
                    kv_ = kvb[si % 2]
                    kn_ = kn[si % 2]
                    P.op("act", lambda e, x_=x_, st_=st_: e.activation(out=junk[:], in_=x_[:], func=AF.Square, accum_out=st_[:, 0:1]),
                         reads=[x_], writes=[junk, st_])
                    for k in range(8):
                        bk = banks[k // 4]
                        P.op("pe", lambda e, k=k, bk=bk, x_=x_: e.transpose(out=bk[:, (k % 4) * 128:(k % 4 + 1) * 128],
                                                                           in_=x_[:, k * 128:(k + 1) * 128], identity=ident[:]),
                             reads=[x_, ident], writes=[bk])
                    P.op("act", lambda e, xT_=xT_: e.activation(out=xT_[:, 0:4, :].rearrange("p a b -> p (a b)"), in_=banks[0][:, :], func=AF.Copy),
                         reads=[banks[0]], writes=[xT_])
                    P.op("dve", lambda e, xT_=xT_: e.tensor_copy(out=xT_[:, 4:8, :].rearrange("p a b -> p (a b)"), in_=banks[1][:, :]),
                         reads=[banks[1]], writes=[xT_])
                    P.op("dve", lambda e, st_=st_: e.tensor_scalar(out=st_[:, 1:2], in0=st_[:, 0:1], scalar1=1.0 / D, scalar2=EPS,
                                                                    op0=ALU.mult, op1=ALU.add), reads=[st_], writes=[st_])
                    P.op("act", lambda e, st_=st_: e.activation(out=st_[:, 2:3], in_=st_[:, 1:2], func=AF.Sqrt), reads=[st_], writes=[st_])
                    P.op("dve", lambda e, st_=st_: e.reciprocal(out=st_[:, 3:4], in_=st_[:, 2:3]), reads=[st_], writes=[st_])

                def stageA2(si):
                    s = p1[si]
                    x_ = xs[si % 4]
                    xT_ = xT[si % 3]
                    st_ = st[si % 3]
                    kv_ = kvb[si % 2]
                    kn_ = kn[si % 2]
                    for half in range(2):
                        bk = banks[2 + half]
                        for k in range(8):
                            P.op("pe", lambda e, k=k, bk=bk, half=half, xT_=xT_: e.matmul(
                                bk[:, 0:384], lhsT=xT_[:, k, :], rhs=Wkv[:, k, half * 384:(half + 1) * 384], start=(k == 0), stop=(k == 7)),
                                reads=[xT_, Wkv], writes=[bk])
                        P.op("dve", lambda e, bk=bk, half=half, kv_=kv_, st_=st_: e.scalar_tensor_tensor(
                            out=kv_[:, half * 384:(half + 1) * 384], in0=bk[:, 0:384], scalar=st_[:, 3:4],
                            in1=BIASkv[:, half * 384:(half + 1) * 384], op0=ALU.mult, op1=ALU.add),
                            reads=[bk, st_, BIASkv], writes=[kv_])
                    ksel_v = kv_[:, 256:768].rearrange("p (a x) -> p a x", a=2)[:, :, 0:128].rearrange("p a (h d) -> p a h d", d=64)
                    P.op("act", lambda e, ksel_v=ksel_v: e.activation(out=sq[:], in_=ksel_v, func=AF.Square), reads=[kv_], writes=[sq])
                    P.op("dve", lambda e, st_=st_: e.tensor_reduce(out=st_[:, 4:8], in_=sq[:].rearrange("p a h d -> p (a h) d"),
                                                                    axis=AX.X, op=ALU.add), reads=[sq], writes=[st_])
                    P.op("dve", lambda e, st_=st_: e.tensor_scalar(out=st_[:, 8:12], in0=st_[:, 4:8], scalar1=1.0 / 64, scalar2=EPS,
                                                                    op0=ALU.mult, op1=ALU.add), reads=[st_], writes=[st_])
                    P.op("act", lambda e, st_=st_: e.activation(out=st_[:, 4:8], in_=st_[:, 8:12], func=AF.Sqrt), reads=[st_], writes=[st_])
                    P.op("dve", lambda e, st_=st_: e.reciprocal(out=st_[:, 12:16], in_=st_[:, 4:8]), reads=[st_], writes=[st_])
                    P.op("dve", lambda e, ksel_v=ksel_v, kn_=kn_, st_=st_: e.tensor_tensor(
                        out=kn_[:], in0=ksel_v,
                        in1=st_[:, 12:16].rearrange("p (a h) -> p a h", a=2).unsqueeze(3).broadcast_to([128, 2, 2, 64]), op=ALU.mult),
                        reads=[kv_, st_], writes=[kn_])

                def stageB(si):
                    s = p1[si]
                    x_ = xs[si % 4]
                    xT_ = xT[si % 3]
                    st_ = st[si % 3]
                    kv_ = kvb[si % 2]
                    kn_ = kn[si % 2]
                    bk = banks[4]
                    P.op("pe", lambda e, kn_=kn_: e.transpose(out=bk[:, 0:128], in_=kn_[:, 0, :, :].rearrange("p h d -> p (h d)"),
                                                               identity=ident[:]), reads=[kn_, ident], writes=[bk])
                    bk5 = banks[5]
                    P.op("pe", lambda e, kv_=kv_: e.transpose(out=bk[:, 256:384], in_=kv_[:, 0:128], identity=ident[:]),
                         reads=[kv_, ident], writes=[bk])
                    P.op("pe", lambda e, kn_=kn_: e.transpose(out=bk5[:, 128:256], in_=kn_[:, 1, :, :].rearrange("p h d -> p (h d)"),
                                                               identity=ident[:]), reads=[kn_, ident], writes=[bk5])
                    P.op("pe", lambda e, kv_=kv_: e.transpose(out=bk5[:, 384:512], in_=kv_[:, 128:256], identity=ident[:]),
                         reads=[kv_, ident], writes=[bk5])
                    P.op("act", lambda e, s=s: e.activation(out=KselT[:, s * 128:(s + 1) * 128], in_=bk[:, 0:128], func=AF.Copy,
                                                             scale=cols[:, 40:41]), reads=[bk, cols], writes=[KselT])
                    kw_ = kwst[si % 2]
                    vw_ = vwst[si % 2]
                    P.op("dve", lambda e, kw_=kw_: e.tensor_scalar(out=kw_[:], in0=bk5[:, 128:256], scalar1=cols[:, 41:42], scalar2=None,
                                                                    op0=ALU.mult), reads=[bk5, cols], writes=[kw_])
                    cj = s // 16
                    cp = (s % 16) * 128
                    P.op("act", lambda e, cj=cj, cp=cp: e.activation(out=cT[0][cj % 2][:, cp:cp + 128], in_=bk[:, 256:384], func=AF.Copy),
                         reads=[bk], writes=[cT[0][cj % 2]])
                    P.op("dve", lambda e, cj=cj, cp=cp: e.tensor_copy(out=cT[1][cj % 2][:, cp:cp + 128], in_=bk5[:, 384:512]),
                         reads=[bk5], writes=[cT[1][cj % 2]])
                    if s == 0:
                        P.op("act", lambda e: e.activation(out=spill0[:, 0, :], in_=bk[:, 256:384], func=AF.Copy), reads=[bk], writes=[spill0])
                        P.op("dve", lambda e: e.tensor_copy(out=spill0[:, 1, :], in_=bk5[:, 384:512]), reads=[bk5], writes=[spill0])
                    if s % 16 == 0 and s > 0:
                        P.op("act", lambda e, cj=cj: e.activation(out=cT[0][(cj - 1) % 2][:, 2048:2176], in_=bk[:, 256:384], func=AF.Copy),
                             reads=[bk], writes=[cT[0][(cj - 1) % 2]])
                        P.op("dve", lambda e, cj=cj: e.tensor_copy(out=cT[1][(cj - 1) % 2][:, 2048:2176], in_=bk5[:, 384:512]),
                             reads=[bk5], writes=[cT[1][(cj - 1) % 2]])
                    P.op("act", lambda e, s=s, kv_=kv_: e.activation(out=Vsel[:, s, :, 0:64],
                                                                       in_=kv_[:, 384:512].rearrange("p (g d) -> p g d", g=2), func=AF.Copy),
                         reads=[kv_], writes=[Vsel])
                    P.op("act", lambda e, vw_=vw_, kv_=kv_: e.activation(out=vw_[:, :, 0:64],
                                                                           in_=kv_[:, 640:768].rearrange("p (g d) -> p g d", g=2), func=AF.Copy),
                         reads=[kv_], writes=[vw_])
                    P.dma("pool", kw_scr[s, :, :], kw_[:], reads=[kw_], writes=[kw_scr_b[s]], owner=kw_)
                    P.dma("pool", vw_scr[s, :, :], vw_[:].rearrange("p g d -> p (g d)"), reads=[vw_], writes=[vw_scr_b[s]], owner=vw_)
                    if s % 16 == 0 and s > 0 and (s - 16) in p1:
                        compress(s // 16 - 1)
                    if s == 127:
                        for kv in range(2):
                            P.op("pool", lambda e, kv=kv: e.tensor_copy(out=cT[kv][1][:, 2048:2176], in_=spill0[:, kv, :]),
                                 reads=[spill0], writes=[cT[kv][1]])
                        compress(7)

                for j in range(min(3, nsl)):
                    loadx(j)
                stageA1(0)
                if nsl > 1:
                    stageA1(1)
                stageA2(0)
                for si in range(nsl):
                    if si + 3 < nsl:
                        loadx(si + 3)
                    if si + 2 < nsl:
                        stageA1(si + 2)
                    if si + 1 < nsl:
                        stageA2(si + 1)
                    stageB(si)
                if dbg:
                    with ExitStack() as el:
                        dstg = P.sb("dstg", [128, 1024], F32, el)

                        def dump(name, buf, ap2d, ncol):
                            d_ = dout(name, [128, ncol])
                            for o in range(0, ncol, 1024):
                                w_ = min(1024, ncol - o)
                                P.op("dve", lambda e, o=o, w_=w_: e.tensor_copy(out=dstg[:, 0:w_], in_=ap2d[:, o:o + w_]),
                                     reads=[buf], writes=[dstg])
                                P.dma("sp", d_[:, o:o + w_], dstg[:, 0:w_], reads=[dstg])
                        dump("d_kselT", KselT, KselT[:], S)
                        dump("d_vsel", Vsel, Vsel[:, 0:NT].rearrange("p a g d -> p (a g d)"), NT * 132)
                        dump("d_kcmpT", KcmpT, KcmpT[:], 1024)
                        dump("d_voc", VOc, VOc[:].rearrange("p a g d -> p (a g d)"), 8 * 2 * 322)
                        P.finish("sp", [dstg])
                    P.barrier()
            if stop_after <= 1:
                P.finish("sp", kw_scr_b + vw_scr_b)
                return nc, dbg_out, P
            P.barrier()
            qt_scr = nc.dram_tensor("qt_scr", [NOWN, 128, 1024], BF16, kind="Internal").ap()
            g_scr = nc.dram_tensor("g_scr", [NOWN, 128, 24], F32, kind="Internal").ap()
            mp_scr = nc.dram_tensor("mp_scr", [NOWN, 128, 512], BF16, kind="Internal").ap()
            qt_b = [Buf(qt_scr, "qts%d" % i) for i in range(NOWN)]
            g_b = [Buf(g_scr, "gs%d" % i) for i in range(NOWN)]
            mp_b = [Buf(mp_scr, "mps%d" % i) for i in range(NOWN)]
            with ExitStack() as e15:
                Wown = P.sb("Wown", [128, 8, 1048], BF16, e15)
                BIASown = P.sb("BIASown", [128, 1048], F32, e15)
                prep_win(0, 1024, Wown, BIASown, e15, 0)
                prep_win(1792, 1816, Wown, BIASown, e15, 1024)
                Wpoolb = P.sb("Wpoolb", [128, 4, 128], BF16, e15)
                PMf = P.sb("PMf", [128, 2, 4, 128], F32, e15)
                PHf = P.sb("PHf", [32, 2, 4, 128], F32, e15)
                with ExitStack() as el:
                    wp_st = P.sb("wp_st", [128, 4, 128], F32, el)
                    P.dma("sp", wp_st[:], wpool_d[:, :, :], writes=[wp_st])
                    P.op("dve", lambda e: e.tensor_copy(out=Wpoolb[:], in_=wp_st[:]), reads=[wp_st], writes=[Wpoolb])
                    P.dma("sp", PMf[:], pm_d[:, :, :, :], writes=[PMf])
                    P.dma("sp", PHf[:], ph_d[:, :, :, :], writes=[PHf])
                P.barrier()
                NB = 3
                fxo = [P.sb("fxo%d" % j, [128, D], F32, e15) for j in range(NB)]
                fxh = [P.sb("fxh%d" % j, [32, D], F32, e15) for j in range(NB)]
                fxT = [P.sb("fxT%d" % j, [128, 8, 128], BF16, e15) for j in range(NB)]
                fxhT = [P.sb("fxhT%d" % j, [128, 8, 32], BF16, e15) for j in range(NB)]
                fst = [P.sb("fst%d" % j, [128, 48], F32, e15) for j in range(NB)]
                fuq = [P.sb("fuq%d" % j, [128, 1048], F32, e15) for j in range(NB)]
                fuh = [P.sb("fuh%d" % j, [32, 512], F32, e15) for j in range(NB)]
                fjunk = P.sb("fjunk", [128, D], F32, e15)
                fsq = P.sb("fsq", [128, 512], F32, e15)
                fqn = P.sb("fqn", [128, 512], F32, e15)
                fpool = P.sb("fpool", [128, 512], BF16, e15)
                fgt = [P.sb("fgt%d" % j, [128, 24], F32, e15) for j in range(NB)]
                fmp = [P.sb("fmp%d" % j, [128, 4, 128], BF16, e15) for j in range(NB)]
                fqt = [P.sb("fqt%d" % j, [128, 2, 512], BF16, e15) for j in range(NB)]
                for j in range(NB):
                    P.op("pool", lambda e, j=j: e.memset(fqt[j][:], 0.0), writes=[fqt[j]])

                def f_s1(ii):
                    i = own[ii]
                    so = 8 * i
                    hs = (so - 1) % NT
                    xo_, xh_, xT_, xhT_, st_ = fxo[ii % NB], fxh[ii % NB], fxT[ii % NB], fxhT[ii % NB], fst[ii % NB]
                    P.dma("sp", xo_[:], xr[so * 128:(so + 1) * 128, :], writes=[xo_])
                    P.dma("sp", xh_[:], xr[hs * 128 + 96:hs * 128 + 128, :], writes=[xh_])
                    P.op("act", lambda e: e.activation(out=fjunk[:], in_=xo_[:], func=AF.Square, accum_out=st_[:, 0:1]),
                         reads=[xo_], writes=[fjunk, st_])
                    P.op("act", lambda e: e.activation(out=fjunk[0:32, :], in_=xh_[:], func=AF.Square, accum_out=st_[0:32, 4:5]),
                         reads=[xh_], writes=[fjunk, st_])
                    for (a, b_) in ((0, 128), (4, 32)):
                        P.op("dve", lambda e, a=a, b_=b_: e.tensor_scalar(out=st_[0:b_, a + 1:a + 2], in0=st_[0:b_, a:a + 1], scalar1=1.0 / D,
                                                                          scalar2=EPS, op0=ALU.mult, op1=ALU.add), reads=[st_], writes=[st_])
                        P.op("act", lambda e, a=a, b_=b_: e.activation(out=st_[0:b_, a + 2:a + 3], in_=st_[0:b_, a + 1:a + 2], func=AF.Sqrt),
                             reads=[st_], writes=[st_])
                        P.op("dve", lambda e, a=a, b_=b_: e.reciprocal(out=st_[0:b_, a + 3:a + 4], in_=st_[0:b_, a + 2:a + 3]),
                             reads=[st_], writes=[st_])
                    for k in range(8):
                        bk = banks[k // 4]
                        P.op("pe", lambda e, k=k, bk=bk: e.transpose(out=bk[:, (k % 4) * 128:(k % 4 + 1) * 128],
                                                                     in_=xo_[:, k * 128:(k + 1) * 128], identity=ident[:]),
                             reads=[xo_, ident], writes=[bk])
                    P.op("act", lambda e: e.activation(out=xT_[:, 0:4, :].rearrange("p a b -> p (a b)"), in_=banks[0][:, :], func=AF.Copy),
                         reads=[banks[0]], writes=[xT_])
                    P.op("dve", lambda e: e.tensor_copy(out=xT_[:, 4:8, :].rearrange("p a b -> p (a b)"), in_=banks[1][:, :]),
                         reads=[banks[1]], writes=[xT_])
                    for k in range(8):
                        P.op("pe", lambda e, k=k: e.transpose(out=banks[2][:, k * 32:(k + 1) * 32], in_=xh_[:, k * 128:(k + 1) * 128],
                                                               identity=ident[0:32, 0:32]), reads=[xh_, ident], writes=[banks[2]])
                    P.op("act", lambda e: e.activation(out=xhT_[:].rearrange("p a b -> p (a b)"), in_=banks[2][:, 0:256], func=AF.Copy),
                         reads=[banks[2]], writes=[xhT_])

                def f_s2a(ii):
                    i = own[ii]
                    pset = 0 if i == 0 else 1
                    xT_, xhT_, st_, uq, uh = fxT[ii % NB], fxhT[ii % NB], fst[ii % NB], fuq[ii % NB], fuh[ii % NB]
                    gt_, mp_, qt_ = fgt[ii % NB], fmp[ii % NB], fqt[ii % NB]
                    for (bi, lo, w_) in ((3, 0, 512), (4, 512, 512), (5, 1024, 24)):
                        bk = banks[bi]
                        for k in range(8):
                            P.op("pe", lambda e, k=k, bk=bk, lo=lo, w_=w_: e.matmul(bk[:, 0:w_], lhsT=xT_[:, k, :], rhs=Wown[:, k, lo:lo + w_],
                                                                                   start=(k == 0), stop=(k == 7)), reads=[xT_, Wown], writes=[bk])
                        P.op("dve", lambda e, bk=bk, lo=lo, w_=w_: e.scalar_tensor_tensor(
                            out=uq[:, lo:lo + w_], in0=bk[:, 0:w_], scalar=st_[:, 3:4], in1=BIASown[:, lo:lo + w_], op0=ALU.mult, op1=ALU.add),
                            reads=[bk, st_, BIASown], writes=[uq])
                    for k in range(8):
                        P.op("pe", lambda e, k=k: e.matmul(banks[6][0:32, 0:512], lhsT=xhT_[:, k, :], rhs=Wown[:, k, 0:512],
                                                           start=(k == 0), stop=(k == 7)), reads=[xhT_, Wown], writes=[banks[6]])
                    P.op("dve", lambda e: e.scalar_tensor_tensor(out=uh[:], in0=banks[6][0:32, 0:512], scalar=st_[0:32, 7:8],
                                                                 in1=BIASown[0:32, 0:512], op0=ALU.mult, op1=ALU.add),
                         reads=[banks[6], st_, BIASown], writes=[uh])
                    P.op("act", lambda e: e.activation(out=gt_[:], in_=uq[:, 1024:1048], func=AF.Sigmoid), reads=[uq], writes=[gt_])

                def f_s2b(ii):
                    i = own[ii]
                    pset = 0 if i == 0 else 1
                    xT_, xhT_, st_, uq, uh = fxT[ii % NB], fxhT[ii % NB], fst[ii % NB], fuq[ii % NB], fuh[ii % NB]
                    gt_, mp_, qt_ = fgt[ii % NB], fmp[ii % NB], fqt[ii % NB]
                    for g in range(4):
                        P.op("pe", lambda e, g=g: e.matmul(banks[7][:, g * 128:(g + 1) * 128], lhsT=uq[:, g * 128:(g + 1) * 128],
                                                           rhs=PMf[:, pset, g, :], start=True, stop=False), reads=[uq, PMf], writes=[banks[7]])
                        P.op("pe", lambda e, g=g: e.matmul(banks[7][:, g * 128:(g + 1) * 128], lhsT=uh[0:32, g * 128:(g + 1) * 128],
                                                           rhs=PHf[0:32, pset, g, :], start=False, stop=True), reads=[uh, PHf], writes=[banks[7]])
                    P.op("act", lambda e: e.activation(out=fpool[:], in_=banks[7][:, :], func=AF.Copy), reads=[banks[7]], writes=[fpool])
                    for g in range(4):
                        P.op("pe", lambda e, g=g: e.matmul(banks[3][:, g * 128:(g + 1) * 128], lhsT=Wpoolb[:, g, :],
                                                           rhs=fpool[:, g * 128:(g + 1) * 128], start=True, stop=True),
                             reads=[Wpoolb, fpool], writes=[banks[3]])
                    for g in range(4):
                        P.op("dve", lambda e, g=g: e.tensor_scalar(out=mp_[:, g, :], in0=banks[3][:, g * 128:(g + 1) * 128],
                                                                    scalar1=cols[:, 44 + g:45 + g], scalar2=None, op0=ALU.mult),
                             reads=[banks[3], cols], writes=[mp_])
                    P.op("act", lambda e: e.activation(out=fsq[:], in_=uq[:, 512:1024], func=AF.Square), reads=[uq], writes=[fsq])
                    P.op("dve", lambda e: e.tensor_reduce(out=st_[:, 8:16], in_=fsq[:].rearrange("p (h d) -> p h d", d=64), axis=AX.X, op=ALU.add),
                         reads=[fsq], writes=[st_])
                    P.op("dve", lambda e: e.tensor_scalar(out=st_[:, 16:24], in0=st_[:, 8:16], scalar1=1.0 / 64, scalar2=EPS,
                                                           op0=ALU.mult, op1=ALU.add), reads=[st_], writes=[st_])
                    P.op("act", lambda e: e.activation(out=st_[:, 24:32], in_=st_[:, 16:24], func=AF.Sqrt), reads=[st_], writes=[st_])
                    P.op("dve", lambda e: e.reciprocal(out=st_[:, 32:40], in_=st_[:, 24:32]), reads=[st_], writes=[st_])
                    P.op("dve", lambda e: e.tensor_tensor(
                        out=fqn[:].rearrange("p (r g d) -> p g r d", g=2, d=64),
                        in0=uq[:, 512:1024].rearrange("p (g r d) -> p g r d", g=2, d=64),
                        in1=st_[:, 32:40].rearrange("p (g r) -> p g r", g=2).unsqueeze(3).broadcast_to([128, 2, 4, 64]), op=ALU.mult),
                        reads=[uq, st_], writes=[fqn])
                    for r in range(4):
                        P.op("pe", lambda e, r=r: e.transpose(out=banks[4][:, r * 128:(r + 1) * 128], in_=fqn[:, r * 128:(r + 1) * 128],
                                                               identity=ident[:]), reads=[fqn, ident], writes=[banks[4]])
                    for g in range(2):
                        P.op("act", lambda e, g=g: e.activation(out=qt_[g * 64:(g + 1) * 64, g, :], in_=banks[4][g * 64:(g + 1) * 64, :], func=AF.Copy,
                                                                scale=cols[g * 64:(g + 1) * 64, 56:57]), reads=[banks[4], cols], writes=[qt_])
                    P.dma("pool", qt_scr[i, :, :], qt_[:].rearrange("p g q -> p (g q)"), reads=[qt_], writes=[qt_b[i]], owner=qt_)
                    P.dma("pool", g_scr[i, :, :], gt_[:], reads=[gt_], writes=[g_b[i]], owner=gt_)
                    P.dma("pool", mp_scr[i, :, :], mp_[:].rearrange("p g t -> p (g t)"), reads=[mp_], writes=[mp_b[i]], owner=mp_)

                no = len(own)
                if no > 0:
                    f_s1(0)
                if no > 1:
                    f_s1(1)
                if no > 0:
                    f_s2a(0)
                for ii in range(no):
                    if ii + 2 < no:
                        f_s1(ii + 2)
                    if ii + 1 < no:
                        f_s2a(ii + 1)
                    f_s2b(ii)
            P.barrier()

            with ExitStack() as e2:
                Woutb = P.sb("Woutb", [128, 8, 1024], BF16, e2)
                AUXL = P.sb("AUXL", [128, 16, 128], BF16, e2)
                ALCL = P.sb("ALCL", [128, 2, 128], BF16, e2)
                TRI = P.sb("TRI", [128, 128], BF16, e2)
                with ExitStack() as el:
                    wo_st = [P.sb("wo_st%d" % j, [128, 1024], F32, el) for j in range(4)]
                    for k in range(8):
                        ws_ = wo_st[k % 4]
                        P.dma("sp", ws_[:], w_out[k * 128:(k + 1) * 128, :], writes=[ws_])
                        P.op("dve", lambda e, k=k, ws_=ws_: e.tensor_tensor(out=Woutb[:, k, :], in0=ws_[:], in1=GA[:, 0:1024], op=ALU.mult),
                             reads=[ws_, GA], writes=[Woutb])
                    P.op("pool", lambda e: e.memset(AUXL[:], 0.0), writes=[AUXL])
                    P.dma("sp", AUXL[0:64], auxl_d[:, :, :], writes=[AUXL])
                    P.op("pool", lambda e: e.memset(ALCL[:], 0.0), writes=[ALCL])
                    P.dma("sp", ALCL[64:68], alcl_d[:, :, :], writes=[ALCL])
                    P.dma("sp", TRI[:], wm_d[:, 1, 4, :], writes=[TRI])
                P.barrier()

                QTz2 = [P.sb("QTz%d" % p, [128, 2, 512], BF16, e2) for p in range(2)]
                gates2 = [P.sb("gates%d" % p, [128, 24], F32, e2) for p in range(2)]
                mixT2 = [P.sb("mixT%d" % p, [128, 8, 128], BF16, e2) for p in range(2)]
                cmb2 = [P.sb("cmb%d" % p, [128, 8, 128], BF16, e2) for p in range(2)]
                AUXR2 = [[P.sb("AUXR%d_%d" % (p, g), [128, 4096], BF16, e2) for g in range(2)] for p in range(2)]
                for p in range(2):
                    for g in range(2):
                        P.op("pool", lambda e, p=p, g=g: e.memset(AUXR2[p][g][:], 0.0), writes=[AUXR2[p][g]])
                attn2 = [P.sb("attn%d" % p, [128, 8, 64], F32, e2) for p in range(2)]
                kwT2 = [P.sb("kwT%d" % p, [128, 5, 128], BF16, e2) for p in range(2)]
                vwb2 = [P.sb("vwb%d" % p, [128, 6, 132], BF16, e2) for p in range(2)]
                for p in range(2):
                    P.op("pool", lambda e, p=p: e.memset(vwb2[p][:, 5, :], 0.0), writes=[vwb2[p]])
                WMt2 = [P.sb("WMt%d" % p, [128, 5, 128], BF16, e2) for p in range(2)]
                t1t = P.sb("t1t", [128, 256], F32, e2)
                PTc = P.sb("PTc", [128, 8, 512], BF16, e2)
                PTb = [P.sb("PTb%d" % j, [128, 512], BF16, e2) for j in range(4)]
                imp = P.sb("imp", [128, 256], F32, e2)
                vv = P.sb("vv", [128, 256], F32, e2)
                vv2 = P.sb("vv2", [128, 256], F32, e2)
                mb2 = P.sb("mb2", [128, 512], F32, e2)
                m8 = P.sb("m8", [128, 24], F32, e2)
                rsc = P.sb("rsc", [128, 8], F32, e2)
                rsb = P.sb("rsb", [128, 16], F32, e2)
                osb = P.sb("osb", [65, 512], F32, e2)
                otmp = P.sb("otmp", [128, 4, 64], F32, e2)
                xo = P.sb("xo", [128, D], F32, e2)

                def front(ii):
                    i = own[ii]
                    p = ii % 2
                    so = 8 * i
                    pset = 0 if i == 0 else 1
                    QTz_, gates, mixT, cmb, AUXR, attn, kwT_, vwb_, WMt = QTz2[p], gates2[p], mixT2[p], cmb2[p], AUXR2[p], attn2[p], kwT2[p], vwb2[p], WMt2[p]
                    P.dma("sp", QTz_[:].rearrange("p g q -> p (g q)"), qt_scr[i, :, :], reads=[qt_b[i]], writes=[QTz_])
                    P.dma("sp", gates[:], g_scr[i, :, :], reads=[g_b[i]], writes=[gates])
                    P.dma("sp", mixT[:, 0:4, :].rearrange("p g t -> p (g t)"), mp_scr[i, :, :], reads=[mp_b[i]], writes=[mixT])
                    P.dma("sp", t1t[:], t1_d[i, :, :], writes=[t1t])
                    chunks = cmp_chunks(i)
                    for cc in chunks:
                        P.dma("sp", cmb[:, cc, :], cmask_d[i, :, cc, :], writes=[cmb], nowaw=True)
                    kas = sorted(set(s_ // 16 for s_ in sel_slots(i)))
                    for g in range(2):
                        for ka in kas:
                            P.dma("sp", AUXR[g][0:5, ka * 512:(ka + 1) * 512], auxr_d[i, g, :, ka * 512:(ka + 1) * 512], writes=[AUXR[g]], nowaw=True)
                        for cc in chunks:
                            P.dma("sp", AUXR[g][64:68, cc * 512:(cc + 1) * 512], alcr_d[i, g, :, cc * 512:(cc + 1) * 512], writes=[AUXR[g]], nowaw=True)
                    if so - 4 >= 0:
                        rb = [kw_scr_b[so - 4 + off] for off in range(5)] + [vw_scr_b[so - 4 + off] for off in range(5)]
                        P.dma("sp", kwT_[:, :, :], kw_scr[so - 4:so + 1, :, :].rearrange("s p t -> p s t"), reads=rb, writes=[kwT_], nowaw=True)
                        P.dma("sp", vwb_[:, 0:5, :], vw_scr[so - 4:so + 1, :, :].rearrange("s p t -> p s t"), reads=rb, writes=[vwb_], nowaw=True)
                    else:
                        for off in range(5):
                            ws = (so - 4 + off) % NT
                            P.dma("sp", kwT_[:, off, :], kw_scr[ws, :, :], reads=[kw_scr_b[ws]], writes=[kwT_], nowaw=True)
                            P.dma("sp", vwb_[:, off, :], vw_scr[ws, :, :], reads=[vw_scr_b[ws]], writes=[vwb_], nowaw=True)
                    P.dma("sp", WMt[:], wm_d[:, pset, :, :], writes=[WMt])
                    yield
                    nch = len(chunks)
                    for g in range(2):
                        bk = banks[6]
                        for ci, cc in enumerate(chunks):
                            P.op("pe", lambda e: e.matmul(bk[:, :], lhsT=KcmpT[:, cc * 128:(cc + 1) * 128], rhs=QTz_[:, g, :],
                                                          start=True, stop=False), reads=[KcmpT, QTz_], writes=[bk])
                            P.op("pe", lambda e: e.matmul(bk[:, :], lhsT=ALCL[:, 1 if cc == 7 else 0, :],
                                                          rhs=AUXR[g][:, cc * 512:(cc + 1) * 512], start=False, stop=False),
                                 reads=[ALCL, AUXR[g]], writes=[bk])
                            for r in range(4):
                                P.op("pe", lambda e, r=r: e.matmul(bk[:, r * 128:(r + 1) * 128], lhsT=identb[:], rhs=cmb[:, cc, :],
                                                                   start=False, stop=(r == 3)), reads=[identb, cmb], writes=[bk])
                            P.op("act", lambda e: e.activation(out=PTc[:, ci, :], in_=bk[:, :], func=AF.Exp), reads=[bk], writes=[PTc])
                            yield
                        bk = banks[7]
                        for r in range(4):
                            h = g * 4 + r
                            for ci, cc in enumerate(chunks):
                                P.op("pe", lambda e: e.matmul(bk[:, 0:321], lhsT=PTc[:, ci, r * 128:(r + 1) * 128], rhs=VOc[:, cc, g, 0:321],
                                                              start=(ci == 0), stop=(ci == nch - 1)), reads=[PTc, VOc], writes=[bk])
                            P.op("dve", lambda e: e.tensor_scalar(out=rsc[:, r:r + 1], in0=bk[:, 64:65], scalar1=1e-30, scalar2=None,
                                                                  op0=ALU.max), reads=[bk], writes=[rsc])
                            P.op("dve", lambda e: e.reciprocal(out=rsc[:, 4 + r:5 + r], in_=rsc[:, r:r + 1]), reads=[rsc], writes=[rsc])
                            P.op("dve", lambda e: e.tensor_scalar(
                                out=attn[:, h, :], in0=bk[:, 0:64], scalar1=rsc[:, 4 + r:5 + r], scalar2=gates[:, 3 * h:3 * h + 1],
                                op0=ALU.mult, op1=ALU.mult), reads=[bk, rsc, gates], writes=[attn])
                            if r == 0:
                                P.op("dve", lambda e: e.tensor_scalar(out=imp[:], in0=bk[:, 65:321], scalar1=rsc[:, 4 + r:5 + r],
                                                                      scalar2=None, op0=ALU.mult), reads=[bk, rsc], writes=[imp])
                            else:
                                P.op("dve", lambda e: e.scalar_tensor_tensor(
                                    out=imp[:], in0=bk[:, 65:321], scalar=rsc[:, 4 + r:5 + r], in1=imp[:], op0=ALU.mult, op1=ALU.add),
                                    reads=[bk, rsc, imp], writes=[imp])
                            yield
                        P.op("dve", lambda e: e.tensor_tensor(out=vv[:], in0=imp[:], in1=t1t[:], op=ALU.add), reads=[imp, t1t], writes=[vv])
                        P.op("dve", lambda e: e.max(out=m8[:, 0:8], in_=vv[:]), reads=[vv], writes=[m8])
                        P.op("dve", lambda e: e.match_replace(out=vv2[:], in_to_replace=m8[:, 0:8], in_values=vv[:], imm_value=-1e30),
                             reads=[vv, m8], writes=[vv2])
                        P.op("dve", lambda e: e.max(out=m8[:, 8:16], in_=vv2[:]), reads=[vv2], writes=[m8])
                        P.op("dve", lambda e: e.tensor_scalar(out=m8[:, 16:17], in0=m8[:, 15:16], scalar1=0.0, scalar2=None, op0=ALU.max),
                             reads=[m8], writes=[m8])
                        mb2v = mb2[:].rearrange("p (a c) -> p a c", c=64)
                        P.op("pool", lambda e: e.memset(mb2v[:, :, 0:32], 0.0), writes=[mb2])
                        P.op("dve", lambda e: e.tensor_scalar(out=mb2v[:, :, 32:64], in0=vv[:].rearrange("p (a c) -> p a c", c=32),
                                                               scalar1=m8[:, 16:17], scalar2=NEG, op0=ALU.is_lt, op1=ALU.mult),
                             reads=[vv, m8], writes=[mb2])
                        yield
                        for hb_ in range(2):
                            bk = banks[6 + hb_]
                            for k4 in range(4):
                                ka = hb_ * 4 + k4
                                P.op("pe", lambda e, ka=ka, k4=k4: e.transpose(out=bk[0:64, k4 * 128:(k4 + 1) * 128],
                                                                               in_=mb2[:, ka * 64:(ka + 1) * 64], identity=ident[:]),
                                     reads=[mb2, ident], writes=[bk])
                            for r in range(4):
                                dst = AUXR[g][32:64, hb_ * 2048:(hb_ + 1) * 2048].rearrange("p (a r q) -> p a r q", r=4, q=128)[:, :, r, :]
                                src = bk[32:64, :].rearrange("p (a q) -> p a q", q=128)
                                if r % 2 == 0:
                                    P.op("act", lambda e, dst=dst, src=src: e.activation(out=dst, in_=src, func=AF.Copy), reads=[bk], writes=[AUXR[g]])
                                else:
                                    P.op("dve", lambda e, dst=dst, src=src: e.tensor_copy(out=dst, in_=src), reads=[bk], writes=[AUXR[g]])
                            yield

                def step(gen):
                    if gen is not None:
                        try:
                            next(gen)
                            return gen
                        except StopIteration:
                            return None
                    return None

                def attend2(ii, slots, kfn, vfn, biasfn, maskfn, branch, bg):
                    p = ii % 2
                    QTz_, gates, attn, AUXR = QTz2[p], gates2[p], attn2[p], AUXR2[p]
                    n = len(slots)
                    SB = [(banks[0], banks[1]), (banks[3], banks[4])]
                    OB = [banks[2], banks[5]]
                    PT = [(PTb[0], PTb[1]), (PTb[2], PTb[3])]

                    def scores(g, idx):
                        s_ = slots[idx]
                        bk = SB[g][idx % 2]
                        kb_, kap = kfn(g, idx, s_)
                        mk = maskfn(g, idx, s_)
                        P.op("pe", lambda e: e.matmul(bk[:, :], lhsT=kap, rhs=QTz_[:, g, :], start=True, stop=False),
                             reads=[kb_, QTz_], writes=[bk])
                        l_, r_ = biasfn(g, idx, s_)
                        P.op("pe", lambda e: e.matmul(bk[:, :], lhsT=l_, rhs=r_, start=False, stop=(mk is None)),
                             reads=[AUXL, AUXR[g]], writes=[bk])
                        if mk is not None:
                            for r in range(4):
                                P.op("pe", lambda e, r=r: e.matmul(bk[:, r * 128:(r + 1) * 128], lhsT=identb[:], rhs=mk[1],
                                                                   start=False, stop=(r == 3)), reads=[identb, mk[0]], writes=[bk])

                    for g in range(2):
                        scores(g, 0)
                    for idx in range(n):
                        for g in range(2):
                            bk = SB[g][idx % 2]
                            pt = PT[g][idx % 2]
                            P.op("act", lambda e, bk=bk, pt=pt: e.activation(out=pt[:], in_=bk[:, :], func=AF.Exp), reads=[bk], writes=[pt])
                        if idx + 1 < n:
                            for g in range(2):
                                scores(g, idx + 1)
                        for g in range(2):
                            pt = PT[g][idx % 2]
                            vb_, vap = vfn(g, idx, slots[idx])
                            P.op("pe", lambda e, vap=vap, pt=pt, g=g: e.matmul(OB[g][:, :], lhsT=vap, rhs=pt[:],
                                                                              start=(idx == 0), stop=(idx == n - 1)),
                                 reads=[vb_, pt], writes=[OB[g]])
                        if idx % 3 == 2:
                            bg = step(bg)
                    for g in range(2):
                        P.op("act", lambda e, g=g: e.activation(out=osb[0:65, :], in_=OB[g][0:65, :], func=AF.Copy), reads=[OB[g]], writes=[osb])
                        bk = OB[g]
                        for r in range(4):
                            P.op("pe", lambda e, r=r: e.transpose(out=bk[:, r * 65:(r + 1) * 65], in_=osb[0:65, r * 128:(r + 1) * 128],
                                                                   identity=ident[0:65, 0:65]), reads=[osb, ident], writes=[bk])
                        ov_ = bk[:, 0:260].rearrange("p (r d) -> p r d", d=65)
                        P.op("dve", lambda e: e.tensor_scalar(out=rsb[:, 8:12], in0=ov_[:, :, 64], scalar1=1e-30, scalar2=None, op0=ALU.max),
                             reads=[bk], writes=[rsb])
                        P.op("dve", lambda e: e.reciprocal(out=rsb[:, 12:16], in_=rsb[:, 8:12]), reads=[rsb], writes=[rsb])
                        gv = gates[:, 12 * g:12 * g + 12].rearrange("p (r b) -> p r b", b=3)[:, :, branch]
                        P.op("dve", lambda e: e.tensor_tensor(out=rsb[:, 8:12], in0=rsb[:, 12:16], in1=gv, op=ALU.mult),
                             reads=[rsb, gates], writes=[rsb])
                        P.op("dve", lambda e: e.tensor_tensor(out=otmp[:], in0=ov_[:, :, 0:64],
                                                               in1=rsb[:, 8:12].unsqueeze(2).broadcast_to([128, 4, 64]), op=ALU.mult),
                             reads=[bk, rsb], writes=[otmp])
                        P.op("dve", lambda e, g=g: e.tensor_tensor(out=attn[:, 4 * g:4 * g + 4, :], in0=attn[:, 4 * g:4 * g + 4, :], in1=otmp[:],
                                                                    op=ALU.add), reads=[attn, otmp], writes=[attn])
                    return bg

                Vflat = Vsel[:].rearrange("p a g d -> p (a g d)")
                gen = front(0) if len(own) > 0 else None
                while gen is not None:
                    gen = step(gen)
                for ii, i in enumerate(own):
                    p = ii % 2
                    so = 8 * i
                    AUXR, kwT_, vwb_, WMt, mixT, attn = AUXR2[p], kwT2[p], vwb2[p], WMt2[p], mixT2[p], attn2[p]
                    vwflat = vwb_[:].rearrange("p a d -> p (a d)")
                    bg = front(ii + 1) if ii + 1 < len(own) else None
                    P.dma("sp", xo[:], xr[so * 128:(so + 1) * 128, :], writes=[xo])
                    bg = attend2(ii, sel_slots(i),
                                 lambda g, idx, s_: (KselT, KselT[:, s_ * 128:(s_ + 1) * 128]),
                                 lambda g, idx, s_: (Vsel, Vflat[:, (s_ * 2 + g) * 66:(s_ * 2 + g) * 66 + 128]),
                                 lambda g, idx, s_: (AUXL[:, s_ % 16, :], AUXR[g][:, (s_ // 16) * 512:(s_ // 16 + 1) * 512]),
                                 lambda g, idx, s_: ((TRI, TRI[:, :]) if s_ == so else None), 1, bg)
                    wslots = [(so - 4 + off) % NT for off in range(5)]
                    bg = attend2(ii, wslots,
                                 lambda g, idx, s_: (kwT_, kwT_[:, idx, :]),
                                 lambda g, idx, s_: (vwb_, vwflat[:, idx * 132 + g * 66:idx * 132 + g * 66 + 128]),
                                 lambda g, idx, s_: (AUXL[0:5, s_ % 16, :], AUXR[g][0:5, (s_ // 16) * 512:(s_ // 16 + 1) * 512]),
                                 lambda g, idx, s_: (WMt, WMt[:, idx, :]), 2, bg)
                    for j in range(4):
                        P.op("pe", lambda e, j=j: e.transpose(out=banks[2][:, j * 128:(j + 1) * 128],
                                                               in_=attn[:, 2 * j:2 * j + 2, :].rearrange("p h d -> p (h d)"), identity=ident[:]),
                             reads=[attn, ident], writes=[banks[2]])
                    P.op("act", lambda e: e.activation(out=mixT[:, 4:8, :].rearrange("p a b -> p (a b)"), in_=banks[2][:, :], func=AF.Copy),
                         reads=[banks[2]], writes=[mixT])
                    for hf in range(2):
                        bk = banks[3 + hf]
                        for k in range(8):
                            P.op("pe", lambda e, k=k, bk=bk, hf=hf: e.matmul(bk[:, :], lhsT=mixT[:, k, :], rhs=Woutb[:, k, hf * 512:(hf + 1) * 512],
                                                                            start=(k == 0), stop=(k == 7)), reads=[mixT, Woutb], writes=[bk])
                        P.op("dve", lambda e, bk=bk, hf=hf: e.tensor_tensor(
                            out=xo[:, hf * 512:(hf + 1) * 512], in0=bk[:, :], in1=xo[:, hf * 512:(hf + 1) * 512], op=ALU.add),
                            reads=[bk, xo], writes=[xo])
                    P.dma("sp", out[i * 128:(i + 1) * 128, :], xo[:], reads=[xo], writes=[out_b[i]], owner=xo)
                    while bg is not None:
                        bg = step(bg)
                if stop_after <= 2:
                    P.finish("sp", [xo] + [out_b[i] for i in own])
                    return nc, dbg_out, P
            P.barrier()
        P.barrier()
        if not do_ffn:
            P.finish("sp", out_b)
            return nc, dbg_out, P
        with ExitStack() as e3:
            x1 = P.sb("x1", [128, len(own), D], F32, e3)
            x1sT = P.sb("x1sT", [128, 8, len(own) * 128], BF16, e3)
            hid = P.sb("hid", [128, 4, len(own) * 128], BF16, e3)
            W1b = [P.sb("W1b%d" % j, [128, 8, 512], BF16, e3) for j in range(2)]
            W2b = [P.sb("W2b%d" % j, [128, 4, 1024], BF16, e3) for j in range(2)]
            w1st = [P.sb("w1st%d" % j, [128, 512], F32, e3) for j in range(8)]
            w2st = [P.sb("w2st%d" % j, [128, 1024], F32, e3) for j in range(4)]
            rst = [P.sb("rst%d" % j, [128, 512], F32, e3) for j in range(2)]
            xsn2 = [P.sb("xsn%d" % j, [128, D], F32, e3) for j in range(2)]
            st32 = [P.sb("st3_%d" % j, [128, 8], F32, e3) for j in range(2)]
            b2c = [P.sb("b2c%d" % j, [128, 4], F32, e3) for j in range(2)]
            ntok = len(own) * 128
            for ii, i in enumerate(own):
                P.dma("sp", x1[:, ii, :], out[i * 128:(i + 1) * 128, :], reads=[out_b[i]], writes=[x1], nowaw=True)
            for ii, i in enumerate(own):
                xsn = xsn2[ii % 2]
                st3 = st32[ii % 2]
                P.op("act", lambda e, ii=ii: e.activation(out=xsn[:], in_=x1[:, ii, :], func=AF.Square, accum_out=st3[:, 0:1]),
                     reads=[x1], writes=[xsn, st3])
                P.op("dve", lambda e: e.tensor_scalar(out=st3[:, 1:2], in0=st3[:, 0:1], scalar1=1.0 / D, scalar2=EPS, op0=ALU.mult, op1=ALU.add),
                     reads=[st3], writes=[st3])
                P.op("act", lambda e: e.activation(out=st3[:, 2:3], in_=st3[:, 1:2], func=AF.Sqrt), reads=[st3], writes=[st3])
                P.op("dve", lambda e: e.reciprocal(out=st3[:, 3:4], in_=st3[:, 2:3]), reads=[st3], writes=[st3])
                P.op("act", lambda e, ii=ii: e.activation(out=xsn[:], in_=x1[:, ii, :], func=AF.Copy, scale=st3[:, 3:4]),
                     reads=[x1, st3], writes=[xsn])
                for k in range(8):
                    bk = banks[k // 4]
                    P.op("pe", lambda e, k=k, bk=bk: e.transpose(out=bk[:, (k % 4) * 128:(k % 4 + 1) * 128], in_=xsn[:, k * 128:(k + 1) * 128],
                                                                 identity=ident[:]), reads=[xsn, ident], writes=[bk])
                for hf in range(2):
                    src = banks[hf][:, :].rearrange("p (a q) -> p a q", q=128)
                    dst = x1sT[:, hf * 4:(hf + 1) * 4, ii * 128:(ii + 1) * 128]
                    if hf == 0:
                        P.op("act", lambda e, src=src, dst=dst: e.activation(out=dst, in_=src, func=AF.Copy), reads=[banks[hf]], writes=[x1sT])
                    else:
                        P.op("dve", lambda e, src=src, dst=dst: e.tensor_copy(out=dst, in_=src), reads=[banks[hf]], writes=[x1sT])
            ntb = (ntok + 511) // 512
            def ffn_load(fg):
                for k in range(8):
                    P.dma("sp", w1st[k][:], w_ff1[k * 128:(k + 1) * 128, fg * 512:(fg + 1) * 512], writes=[w1st[k]])
                for fc in range(4):
                    r0 = (fg * 4 + fc) * 128
                    P.dma("sp", w2st[fc][:], w_ff2[r0:r0 + 128, :], writes=[w2st[fc]])

            def ffn_conv(fg):
                W1_ = W1b[fg % 2]
                W2_ = W2b[fg % 2]
                b2_ = b2c[fg % 2]
                for k in range(8):
                    ws_ = w1st[k]
                    P.op("act", lambda e, k=k, ws_=ws_, W1_=W1_: e.activation(out=W1_[:, k, :], in_=ws_[:], func=AF.Copy, scale=cols[:, 32 + k:33 + k]),
                         reads=[ws_, cols], writes=[W1_])
                    for fc in range(4):
                        P.op("pe", lambda e, k=k, fc=fc, ws_=ws_: e.matmul(banks[4 + fc][:, 0:1], lhsT=ws_[:, fc * 128:(fc + 1) * 128],
                                                                          rhs=modcol[:, 16 + k:17 + k], start=(k == 0), stop=(k == 7)),
                             reads=[ws_, modcol], writes=[banks[4 + fc]])
                for fc in range(4):
                    P.op("dve", lambda e, fc=fc, b2_=b2_: e.tensor_copy(out=b2_[:, fc:fc + 1], in_=banks[4 + fc][:, 0:1]), reads=[banks[4 + fc]], writes=[b2_])
                for fc in range(4):
                    ws_ = w2st[fc]
                    P.op("dve", lambda e, fc=fc, ws_=ws_, W2_=W2_: e.tensor_tensor(out=W2_[:, fc, :], in0=ws_[:], in1=GA[:, 1024:2048], op=ALU.mult),
                         reads=[ws_, GA], writes=[W2_])

            ffn_load(0)
            ffn_conv(0)
            for fg in range(8):
                W1_ = W1b[fg % 2]
                W2_ = W2b[fg % 2]
                b2_ = b2c[fg % 2]
                if fg + 1 < 8:
                    ffn_load(fg + 1)
                cnt_ = 0
                for fc in range(4):
                    for tb in range(ntb):
                        t0_ = tb * 512
                        tw = min(512, ntok - t0_)
                        bk = banks[cnt_ % 2]
                        r_ = rst[cnt_ % 2]
                        for k in range(8):
                            P.op("pe", lambda e, k=k, fc=fc, bk=bk, t0_=t0_, tw=tw: e.matmul(
                                bk[:, 0:tw], lhsT=W1_[:, k, fc * 128:(fc + 1) * 128], rhs=x1sT[:, k, t0_:t0_ + tw], start=(k == 0), stop=(k == 7)),
                                reads=[W1_, x1sT], writes=[bk])
                        P.op("act", lambda e, fc=fc, bk=bk, r_=r_, tw=tw: e.activation(out=r_[:, 0:tw], in_=bk[:, 0:tw], func=AF.Relu, bias=b2_[:, fc:fc + 1]),
                             reads=[bk, b2_], writes=[r_])
                        P.op("dve", lambda e, fc=fc, r_=r_, t0_=t0_, tw=tw: e.tensor_tensor(
                            out=hid[:, fc, t0_:t0_ + tw], in0=r_[:, 0:tw], in1=r_[:, 0:tw], op=ALU.mult), reads=[r_], writes=[hid])
                        cnt_ += 1
                if fg + 1 < 8:
                    ffn_conv(fg + 1)
                for ii in range(len(own)):
                    for hf in range(2):
                        bk = banks[(ii * 2 + hf) % 4]
                        for fc in range(4):
                            P.op("pe", lambda e, fc=fc, bk=bk, ii=ii, hf=hf: e.matmul(
                                bk[:, :], lhsT=hid[:, fc, ii * 128:(ii + 1) * 128], rhs=W2_[:, fc, hf * 512:(hf + 1) * 512], start=(fc == 0), stop=(fc == 3)),
                                reads=[hid, W2_], writes=[bk])
                        P.op("dve", lambda e, bk=bk, ii=ii, hf=hf: e.tensor_tensor(
                            out=x1[:, ii, hf * 512:(hf + 1) * 512], in0=bk[:, :], in1=x1[:, ii, hf * 512:(hf + 1) * 512], op=ALU.add),
                            reads=[bk, x1], writes=[x1])
            for ii, i in enumerate(own):
                P.dma("sp", out[i * 128:(i + 1) * 128, :], x1[:, ii, :], reads=[x1], writes=[out_b[i]], owner=x1)
            P.finish("sp", [x1] + out_b)
            P.barrier()
    return nc, dbg_out, P


def make_inputs(inp, c):
    f = lambda a: np.ascontiguousarray(np.asarray(a, np.float32))
    x = np.asarray(inp["x"], np.float32)[0]
    m = {}
    m["xr"] = f(np.roll(x.reshape(NT, 128, D), -c, axis=0).reshape(S, D))
    m["ccol"] = f(np.asarray(inp["c"])[0].reshape(8, 128).T)
    m["w_ada"] = f(inp["w_ada"][0])
    m["b_ada"] = f(np.asarray(inp["b_ada"])[0][None, :])
    m["g1col"] = f(np.asarray(inp["norm1_g"])[0].reshape(8, 128).T)
    m["g2col"] = f(np.asarray(inp["norm2_g"])[0].reshape(8, 128).T)
    m["w_in"] = f(inp["w_in"][0])
    m["w_pool"] = f(np.asarray(inp["w_pool"])[0].transpose(1, 0, 2))
    sc = np.zeros((128, 16), np.float32)
    for j, nm in enumerate(("ks_gain", "kw_gain", "kc_gain", "q_gain")):
        sc[:, j] = np.tile(np.asarray(inp[nm])[0], 2)
    sc[:, 4:8] = np.asarray(inp["pool_scale"])[0].reshape(4, 128).T
    sc[:, 8:10] = np.asarray(inp["cmp_b1_k"])[0].reshape(2, 128).T
    sc[:, 10:12] = np.asarray(inp["cmp_b1_v"])[0].reshape(2, 128).T
    m["smallc"] = sc
    m["cmp_w1_k"] = f(np.asarray(inp["cmp_w1_k"])[0].reshape(32, 64, 256).transpose(1, 0, 2))
    m["cmp_w1_v"] = f(np.asarray(inp["cmp_w1_v"])[0].reshape(32, 64, 256).transpose(1, 0, 2))
    w2 = np.stack([np.asarray(inp["cmp_w2_k"])[0], np.asarray(inp["cmp_w2_v"])[0]], 0)
    m["cmp_w2"] = f(w2.reshape(2, 2, 128, 64).transpose(2, 0, 1, 3))
    m["cmp_b2"] = f(np.concatenate([np.asarray(inp["cmp_b2_k"])[0], np.asarray(inp["cmp_b2_v"])[0]])[None, :])
    m["cmp_b1row"] = f(np.concatenate([np.asarray(inp["cmp_b1_k"])[0], np.asarray(inp["cmp_b1_v"])[0]])[None, :])
    pos = np.stack([np.asarray(inp["cmp_pos_k"])[0], np.asarray(inp["cmp_pos_v"])[0]], 0)
    m["cmp_posT"] = f(pos.transpose(2, 0, 1))
    m["w_out"] = f(inp["w_out"][0])
    m["w_ff1"] = f(inp["w_ff1"][0])
    m["w_ff2"] = f(inp["w_ff2"][0])
    bf = ("ov", "auxl", "auxr", "alcr", "cmask", "alcl", "wm")
    for k, v in shared_tables().items():
        m[k] = v.astype(ml_dtypes.bfloat16) if k in bf else v
    for k, v in core_tables(c).items():
        m[k] = v.astype(ml_dtypes.bfloat16) if k in bf else v
    return m


_CACHE = {}


def kernel(**inputs):
    if "nc" not in _CACHE:
        _CACHE["nc"] = build({})[0]
    nc = _CACHE["nc"]
    in_maps = [make_inputs(inputs, c) for c in range(NCORES)]
    res = run_bass_kernel_spmd(nc, in_maps, core_ids=list(range(NCORES)))
    full = np.zeros((NT, 128, D), np.float32)
    for c in range(NCORES):
        o = np.asarray(res.results[c]["out"]).reshape(NOWN, 128, D)
        for i in range(NOWN):
            full[c + 8 * i] = o[i]
    return full.reshape(1, S, D)
```
